# Optimizing a Trainium2 kernel written in Bass

```python
import math
import jax
import jax.numpy as jnp
from jax import lax
import numpy as np

D_MODEL = 4096
BATCH = 2
SEQ = 8192
DEPTH = 2

EPS = 1e-6
SSD_W = 2048
SSD_HEAD_DIM = 64
SSD_HEADS = SSD_W // SSD_HEAD_DIM
SSD_GROUPS = 8
SSD_STATE = 128
SSD_CONV = 4
SSD_CHUNK = 128
SSD_CONV_DIM = SSD_W + 2 * SSD_GROUPS * SSD_STATE
SWA_W = 1024
SWA_HEAD_DIM = 64
SWA_HEADS = SWA_W // SWA_HEAD_DIM
SWA_KV_HEADS = 4
WINDOW = 128
REL_BUCKETS = 32
REL_MAX_DIST = 128
GLA_W = 1024
GLA_HEADS = 4
GLA_DV = GLA_W // GLA_HEADS
GLA_DK = GLA_DV // 2
GLA_K_TOT = GLA_HEADS * GLA_DK
GLA_RANK = 16
GLA_TAU = 16.0
GLA_CHUNK = 64
D_MIX = SSD_W + SWA_W + GLA_W
D_FF = 11008
FFN_CONV = 3
SPLIT_SIZES = (SSD_W, SSD_CONV_DIM, SSD_HEADS,
               SWA_W, SWA_KV_HEADS * SWA_HEAD_DIM, SWA_KV_HEADS * SWA_HEAD_DIM,
               GLA_K_TOT, GLA_K_TOT, GLA_W, GLA_W, GLA_RANK)
D_IN = sum(SPLIT_SIZES)

kernel_name = "hybrid_ssd_swa_gla_convffn"


def rmsnorm(x, w):
    xf = x.astype(jnp.float32)
    y = xf * lax.rsqrt(jnp.mean(xf * xf, axis=-1, keepdims=True) + EPS)
    return (y * w.astype(jnp.float32)).astype(x.dtype)


def causal_dwconv(x, w, b):
    K = w.shape[0]
    S = x.shape[1]
    xp = jnp.pad(x, ((0, 0), (K - 1, 0), (0, 0)))
    y = b
    for i in range(K):
        y = y + xp[:, i:i + S] * w[i]
    return y


def t5_bucket(dist):
    max_exact = REL_BUCKETS // 2
    d = jnp.maximum(dist.astype(jnp.float32), 1.0)
    large = max_exact + (jnp.log(d / max_exact) / math.log(REL_MAX_DIST / max_exact)
                         * (REL_BUCKETS - max_exact)).astype(jnp.int32)
    large = jnp.minimum(large, REL_BUCKETS - 1)
    return jnp.where(dist < max_exact, dist, large)


def t5_window_bias(rel_bias):
    qi = jnp.arange(WINDOW)[:, None]
    kj = jnp.arange(2 * WINDOW)[None, :]
    dist = jnp.clip(qi + WINDOW - kj, 0, WINDOW - 1)
    b = rel_bias[t5_bucket(dist)]
    b = jnp.transpose(b, (2, 0, 1))
    return b.reshape(SWA_KV_HEADS, SWA_HEADS // SWA_KV_HEADS, WINDOW, 2 * WINDOW).astype(jnp.float32)


def ssd_chunked(x, dt, A, Bm, Cm):
    b, s, h, p = x.shape
    g, n = Bm.shape[2], Bm.shape[3]
    r = h // g
    L = SSD_CHUNK
    c = s // L
    xd = (x * dt[..., None]).reshape(b, c, L, g, r, p)
    a_cs = jnp.cumsum((dt * A).reshape(b, c, L, g, r), axis=2)
    Bc = Bm.reshape(b, c, L, g, n)
    Cc = Cm.reshape(b, c, L, g, n)
    tril = jnp.tril(jnp.ones((L, L), bool))
    seg = a_cs[:, :, :, None] - a_cs[:, :, None, :]
    decay = jnp.exp(jnp.where(tril[:, :, None, None], seg, -jnp.inf))
    cb = jnp.einsum('bclgn,bcsgn->bclsg', Cc, Bc)
    y_diag = jnp.einsum('bclsgr,bcsgrp->bclgrp', cb[..., None] * decay, xd)
    decay_to_end = jnp.exp(a_cs[:, :, -1:] - a_cs)
    states = jnp.einsum('bcsgn,bcsgrp->bcgrpn', Bc, xd * decay_to_end[..., None])
    chunk_decay = jnp.exp(a_cs[:, :, -1])

    def step(hs, inp):
        st, dec = inp
        return hs * dec[..., None, None] + st, hs

    h0 = jnp.zeros((b, g, r, p, n), jnp.float32)
    _, prev = lax.scan(step, h0, (jnp.moveaxis(states, 1, 0), jnp.moveaxis(chunk_decay, 1, 0)))
    prev = jnp.moveaxis(prev, 0, 1)
    y_off = jnp.einsum('bclgn,bcgrpn->bclgrp', Cc, prev) * jnp.exp(a_cs)[..., None]
    return (y_diag + y_off).reshape(b, s, h, p)


def ssd_mixer(z, xbc, dt_raw, conv_w, conv_b, dt_bias, a_log, d_skip, norm_w):
    Bsz, S, _ = z.shape
    f32 = jnp.float32
    xbc = jax.nn.silu(causal_dwconv(xbc, conv_w, conv_b))
    xs, Bm, Cm = jnp.split(xbc, [SSD_W, SSD_W + SSD_GROUPS * SSD_STATE], axis=-1)
    xs = xs.astype(f32).reshape(Bsz, S, SSD_HEADS, SSD_HEAD_DIM)
    Bm = Bm.astype(f32).reshape(Bsz, S, SSD_GROUPS, SSD_STATE)
    Cm = Cm.astype(f32).reshape(Bsz, S, SSD_GROUPS, SSD_STATE)
    dt = jax.nn.softplus(dt_raw.astype(f32) + dt_bias.astype(f32))
    A = -jnp.exp(a_log.astype(f32))
    y = ssd_chunked(xs, dt, A, Bm, Cm) + xs * d_skip.astype(f32)[:, None]
    y = y.reshape(Bsz, S, SSD_W) * jax.nn.silu(z.astype(f32))
    yg = y.reshape(Bsz, S, SSD_GROUPS, SSD_W // SSD_GROUPS)
    yg = yg * lax.rsqrt(jnp.mean(yg * yg, axis=-1, keepdims=True) + EPS)
    return (yg.reshape(Bsz, S, SSD_W) * norm_w.astype(f32)).astype(z.dtype)


def swa_mixer(q, k, v, sinks, bias, norm_w):
    Bsz, S, _ = q.shape
    W = WINDOW
    nb = S // W
    G = SWA_HEADS // SWA_KV_HEADS
    f32 = jnp.float32
    qb = q.reshape(Bsz, nb, W, SWA_KV_HEADS, G, SWA_HEAD_DIM)
    k = k.reshape(Bsz, S, SWA_KV_HEADS, SWA_HEAD_DIM)
    v = v.reshape(Bsz, S, SWA_KV_HEADS, SWA_HEAD_DIM)

    def band(t):
        prev = jnp.pad(t, ((0, 0), (W, 0), (0, 0), (0, 0)))[:, :S]
        shp = (Bsz, nb, W, SWA_KV_HEADS, SWA_HEAD_DIM)
        return jnp.concatenate([prev.reshape(shp), t.reshape(shp)], axis=2)

    kb, vb = band(k), band(v)
    s = jnp.einsum('bnqhgd,bnkhd->bnhgqk', qb, kb).astype(f32) * (SWA_HEAD_DIM ** -0.5) + bias
    qi = jnp.arange(W)[:, None]
    kj = jnp.arange(2 * W)[None, :]
    dist = qi + W - kj
    valid = (dist >= 0) & (dist < W)
    valid_blk = jnp.where((jnp.arange(nb) == 0)[:, None, None], valid & (kj >= W), valid)
    s = jnp.where(valid_blk[None, :, None, None], s, -jnp.inf)
    sk = sinks.astype(f32).reshape(SWA_KV_HEADS, G)[None, None, :, :, None, None]
    m = jnp.maximum(jnp.max(s, axis=-1, keepdims=True), sk)
    p = jnp.exp(s - m)
    p = p / (jnp.sum(p, axis=-1, keepdims=True) + jnp.exp(sk - m))
    o = jnp.einsum('bnhgqk,bnkhd->bnqhgd', p.astype(v.dtype), vb).reshape(Bsz, S, SWA_W)
    return rmsnorm(o, norm_w)


def gla_mixer(q, k, v, g_out, g_lr, w_gate, b_gate, norm_w):
    Bsz, S, _ = q.shape
    L = GLA_CHUNK
    c = S // L
    f32 = jnp.float32
    log_a = jax.nn.log_sigmoid(g_lr.astype(f32) @ w_gate.astype(f32) + b_gate.astype(f32)) / GLA_TAU
    shp_k = (Bsz, c, L, GLA_HEADS, GLA_DK)
    q = q.astype(f32).reshape(shp_k) * (GLA_DK ** -0.5)
    k = k.astype(f32).reshape(shp_k)
    v = v.astype(f32).reshape(Bsz, c, L, GLA_HEADS, GLA_DV)
    b_cum = jnp.cumsum(log_a.reshape(shp_k), axis=2)
    q_dec = q * jnp.exp(b_cum)
    k_inv = k * jnp.exp(-b_cum)
    causal = jnp.tril(jnp.ones((L, L), bool))
    att = jnp.where(causal, jnp.einsum('bclhk,bcshk->bchls', q_dec, k_inv), 0.0)
    o = jnp.einsum('bchls,bcshv->bclhv', att, v)
    k_end = k * jnp.exp(b_cum[:, :, -1:] - b_cum)
    states = jnp.einsum('bcshk,bcshv->bchkv', k_end, v)
    decay = jnp.exp(b_cum[:, :, -1])

    def step(st_c, inp):
        st, dec = inp
        return st_c * dec[..., None] + st, st_c

    s0 = jnp.zeros((Bsz, GLA_HEADS, GLA_DK, GLA_DV), f32)
    _, prev = lax.scan(step, s0, (jnp.moveaxis(states, 1, 0), jnp.moveaxis(decay, 1, 0)))
    prev = jnp.moveaxis(prev, 0, 1)
    o = o + jnp.einsum('bclhk,bchkv->bclhv', q_dec, prev)
    o = o.reshape(Bsz, S, GLA_HEADS, GLA_DV)
    o = o * lax.rsqrt(jnp.mean(o * o, axis=-1, keepdims=True) + EPS) * norm_w.astype(f32)
    o = o.reshape(Bsz, S, GLA_W) * jax.nn.silu(g_out.astype(f32))
    return o.astype(g_out.dtype)


def setup_inputs(seed: int = 0) -> dict:
    key = jax.random.key(seed)
    ks = jax.random.split(key, 24)
    f32 = jnp.float32
    Ld = DEPTH

    def nrm(k, shape, scale):
        return jax.random.normal(k, shape, f32) * scale

    def gain(k, shape):
        return 1.0 + 0.02 * jax.random.normal(k, shape, f32)

    dt0 = jnp.exp(jax.random.uniform(ks[5], (Ld, SSD_HEADS), f32, math.log(1e-3), math.log(1e-1)))
    dt_bias = dt0 + jnp.log(-jnp.expm1(-dt0))
    a_log = jnp.log(jax.random.uniform(ks[6], (Ld, SSD_HEADS), f32, 1.0, 16.0))
    return {
        "x": nrm(ks[0], (BATCH, SEQ, D_MODEL), 1.0),
        "attn_norm": gain(ks[1], (Ld, D_MODEL)),
        "w_in": nrm(ks[2], (Ld, D_MODEL, D_IN), D_MODEL ** -0.5),
        "ssd_conv_w": nrm(ks[3], (Ld, SSD_CONV, SSD_CONV_DIM), SSD_CONV ** -0.5),
        "ssd_conv_b": nrm(ks[4], (Ld, SSD_CONV_DIM), 0.02),
        "ssd_dt_bias": dt_bias,
        "ssd_a_log": a_log,
        "ssd_d": 1.0 + 0.1 * jax.random.normal(ks[7], (Ld, SSD_HEADS), f32),
        "ssd_norm": gain(ks[8], (Ld, SSD_W)),
        "swa_sinks": nrm(ks[9], (Ld, SWA_HEADS), 1.0),
        "swa_norm": gain(ks[10], (Ld, SWA_W)),
        "gla_w_gate": nrm(ks[11], (Ld, GLA_RANK, GLA_K_TOT), GLA_RANK ** -0.5),
        "gla_b_gate": nrm(ks[12], (Ld, GLA_K_TOT), 0.5),
        "gla_norm": gain(ks[13], (Ld, GLA_DV)),
        "w_out": nrm(ks[14], (Ld, D_MIX, D_MODEL), D_MIX ** -0.5),
        "ffn_norm": gain(ks[15], (Ld, D_MODEL)),
        "w_gate": nrm(ks[16], (Ld, D_MODEL, D_FF), D_MODEL ** -0.5),
        "w_up": nrm(ks[17], (Ld, D_MODEL, D_FF), D_MODEL ** -0.5),
        "ffn_conv_w": nrm(ks[18], (Ld, FFN_CONV, D_FF), FFN_CONV ** -0.5),
        "ffn_conv_b": nrm(ks[19], (Ld, D_FF), 0.02),
        "w_down": nrm(ks[20], (Ld, D_FF, D_MODEL), D_FF ** -0.5),
        "rel_bias": nrm(ks[21], (REL_BUCKETS, SWA_HEADS), 0.5),
        "final_norm": gain(ks[22], (D_MODEL,)),
    }


def reference(x, attn_norm, w_in, ssd_conv_w, ssd_conv_b, ssd_dt_bias, ssd_a_log, ssd_d, ssd_norm,
              swa_sinks, swa_norm, gla_w_gate, gla_b_gate, gla_norm, w_out, ffn_norm, w_gate, w_up,
              ffn_conv_w, ffn_conv_b, w_down, rel_bias, final_norm):
    bias = t5_window_bias(rel_bias)
    split_at = [int(i) for i in np.cumsum(SPLIT_SIZES)[:-1]]
    for l in range(DEPTH):
        h = rmsnorm(x, attn_norm[l])
        proj = h @ w_in[l]
        (z, xbc, dt_raw, sq, sk, sv, gq, gk, gv, gg, glr) = jnp.split(proj, split_at, axis=-1)
        y_a = ssd_mixer(z, xbc, dt_raw, ssd_conv_w[l], ssd_conv_b[l], ssd_dt_bias[l],
                        ssd_a_log[l], ssd_d[l], ssd_norm[l])
        y_b = swa_mixer(sq, sk, sv, swa_sinks[l], bias, swa_norm[l])
        y_c = gla_mixer(gq, gk, gv, gg, glr, gla_w_gate[l], gla_b_gate[l], gla_norm[l])
        x = x + jnp.concatenate([y_a, y_b, y_c], axis=-1) @ w_out[l]
        h = rmsnorm(x, ffn_norm[l])
        gate = causal_dwconv(h @ w_gate[l], ffn_conv_w[l], ffn_conv_b[l])
        x = x + (jax.nn.silu(gate) * (h @ w_up[l])) @ w_down[l]
    return rmsnorm(x, final_norm)
```

```python
import math
from contextlib import ExitStack

import numpy as np
import concourse.bass as bass
import concourse.mybir as mybir
from concourse.bass_utils import run_bass_kernel_spmd

F32 = mybir.dt.float32
BF16 = mybir.dt.bfloat16
AF = mybir.ActivationFunctionType
ALU = mybir.AluOpType

D = 4096
DFF = 11008
NHT = DFF // 128
EPS = 1e-6


class Buf:
    __slots__ = ("t", "w", "r", "name")

    def __init__(self, t, name):
        self.t = t
        self.w = None
        self.r = {}
        self.name = name


class Ctx:
    def __init__(self, nc, es, n_dma_sems=24):
        self.nc = nc
        self.es = es
        self.E = {"pe": nc.tensor, "act": nc.scalar, "dve": nc.vector, "pool": nc.gpsimd, "sp": nc.sync}
        self.sems = {}
        self.cnt = {}
        for k in ("pe", "act", "dve", "pool"):
            self.sems[k] = es.enter_context(nc.semaphore("s_" + k))
            self.cnt[k] = 0
        self.ndma = n_dma_sems
        for i in range(n_dma_sems):
            self.sems[("d", i)] = es.enter_context(nc.semaphore("s_d%d" % i))
            self.cnt[("d", i)] = 0
        self.rr = 0
        self.seen = {k: {} for k in self.E}
        self.nbuf = 0

    def sbuf(self, shape, dt, name=None):
        self.nbuf += 1
        name = "sb_" + (name or str(self.nbuf))
        return Buf(self.es.enter_context(self.nc.sbuf_tensor(name, list(shape), dt)), name)

    def psum(self, shape, dt, name=None):
        self.nbuf += 1
        name = "ps_" + (name or str(self.nbuf))
        return Buf(self.es.enter_context(self.nc.psum_tensor(name, list(shape), dt)), name)

    def _wait(self, eng, deps):
        E = self.E[eng]
        seen = self.seen[eng]
        best = {}
        for (k, v) in deps:
            if v > best.get(k, 0):
                best[k] = v
        for k, v in best.items():
            if k == "pe" and eng == "pe":
                continue
            if seen.get(k, 0) >= v:
                continue
            E.wait_ge(self.sems[k], v)
            seen[k] = v

    @staticmethod
    def _deps(reads, writes):
        deps = []
        for b in reads:
            if b.w is not None:
                deps.append(b.w)
        for b in writes:
            if b.w is not None:
                deps.append(b.w)
            deps.extend(b.r.items())
        return deps

    @staticmethod
    def _commit(ev, reads, writes):
        k, v = ev
        for b in reads:
            if b.r.get(k, 0) < v:
                b.r[k] = v
        for b in writes:
            b.w = ev
            b.r = {}

    def op(self, eng, reads, writes, fn):
        self._wait(eng, self._deps(reads, writes))
        ins = fn(self.E[eng])
        if isinstance(ins, (list, tuple)):
            ins = ins[-1]
        self.cnt[eng] += 1
        ins.then_inc(self.sems[eng], 1)
        self._commit((eng, self.cnt[eng]), reads, writes)

    def dma(self, q, out_ap, in_ap, reads, writes):
        i = self.rr
        self.rr = (self.rr + 1) % self.ndma
        k = ("d", i)
        deps = self._deps(reads, writes)
        if self.cnt[k] > 0:
            deps.append((k, self.cnt[k]))
        self._wait(q, deps)
        ins = self.E[q].dma_start(out=out_ap, in_=in_ap)
        self.cnt[k] += 16
        ins.then_inc(self.sems[k], 16)
        self._commit((k, self.cnt[k]), reads, writes)

    def finish(self, bufs):
        deps = []
        for b in bufs:
            if b.w is not None:
                deps.append(b.w)
            deps.extend(b.r.items())
        self._wait("sp", deps)


class Ring:
    def __init__(self, bufs):
        self.bufs = bufs
        self.i = 0

    def next(self):
        b = self.bufs[self.i % len(self.bufs)]
        self.i += 1
        return b


def emit_rstd(cx, ss, ms, sd, rstd, dim):
    cx.op("dve", [ss], [ms], lambda E: E.tensor_scalar(out=ms.t[:, 0:1], in0=ss.t[:, 0:1], scalar1=1.0 / dim,
                                                       scalar2=EPS, op0=ALU.mult, op1=ALU.add))
    cx.op("act", [ms], [sd], lambda E: E.activation(out=sd.t[:, 0:1], in_=ms.t[:, 0:1], func=AF.Sqrt))
    cx.op("dve", [sd], [rstd], lambda E: E.reciprocal(out=rstd.t[:, 0:1], in_=sd.t[:, 0:1]))


def emit_transposeT(cx, hb, actT, s, gain, identb, tps):
    for g in range(4):
        pt = tps.next()

        def tr(E, pt=pt, g=g):
            r = None
            for j in range(8):
                c0 = (g * 8 + j) * 128
                r = E.transpose(out=pt.t[:, j * 128:(j + 1) * 128], in_=hb.t[:, c0:c0 + 128], identity=identb.t[:])
            return r

        cx.op("pe", [hb, identb], [pt], tr)
        cx.op("dve", [pt, gain], [actT], lambda E, pt=pt, g=g: E.tensor_tensor(
            out=actT.t[:, g * 8:(g + 1) * 8, s * 128:(s + 1) * 128],
            in0=pt.t[:].rearrange("p (a b) -> p a b", a=8),
            in1=gain.t[:, g * 8:(g + 1) * 8].unsqueeze(2).broadcast_to([128, 8, 128]),
            op=ALU.mult))


def build_k2(ntok):
    nc = bass.Bass("TRN2", target_bir_lowering=False)
    x = nc.dram_tensor("x", [128 + ntok, D], F32, kind="ExternalInput").ap()
    mix = nc.dram_tensor("mix", [128 + ntok, D], F32, kind="ExternalInput").ap()
    w_out = nc.dram_tensor("w_out", [D, D], F32, kind="ExternalInput").ap()
    w_gate = nc.dram_tensor("w_gate", [D, DFF], F32, kind="ExternalInput").ap()
    w_up = nc.dram_tensor("w_up", [D, DFF], F32, kind="ExternalInput").ap()
    w_down = nc.dram_tensor("w_down", [DFF, D], F32, kind="ExternalInput").ap()
    gains_d = nc.dram_tensor("gains", [128, 64], F32, kind="ExternalInput").ap()
    convw_d = nc.dram_tensor("convw", [128, NHT * 4], F32, kind="ExternalInput").ap()
    ident_d = nc.dram_tensor("ident", [128, 128], F32, kind="ExternalInput").ap()
    fnorm_d = nc.dram_tensor("fnorm", [D], F32, kind="ExternalInput").ap()
    flags_d = nc.dram_tensor("flags", [128, 2], F32, kind="ExternalInput").ap()
    y = nc.dram_tensor("y", [ntok, D], F32, kind="ExternalOutput").ap()

    with ExitStack() as es:
        cx = Ctx(nc, es)
        xs = [cx.sbuf([128, D], F32, "xs%d" % i) for i in range(4)]
        hbs = Ring([cx.sbuf([128, D], BF16, "hb%d" % i) for i in range(2)])
        actT = cx.sbuf([128, 32, 512], BF16, "actT")
        hidT = cx.sbuf([128, 43 * 512], BF16, "hidT")
        wring = Ring([cx.sbuf([128, 4096], BF16, "wp%d" % i) for i in range(3)])
        gbufs = Ring([cx.sbuf([128, 516], F32, "gb%d" % i) for i in range(2)])
        a1s = Ring([cx.sbuf([128, 512], F32, "a1%d" % i) for i in range(2)])
        sgs = Ring([cx.sbuf([128, 512], F32, "sg%d" % i) for i in range(2)])
        carry = cx.sbuf([128, NHT * 2], F32, "carry")
        convw = cx.sbuf([128, NHT * 4], F32, "convw")
        gain_mix = cx.sbuf([128, 32], F32, "gain_mix")
        gain_ffn = cx.sbuf([128, 32], F32, "gain_ffn")
        identb = cx.sbuf([128, 128], BF16, "identb")
        smalls = Ring([cx.sbuf([128, 4], F32, "sm%d" % i) for i in range(16)])
        acc = [cx.psum([128, 512], F32, "acc%d" % i) for i in range(6)]
        tps = Ring([cx.psum([128, 1024], BF16, "tp%d" % i) for i in range(2)])
        outbuf = cx.sbuf([1, 4], F32, "outdummy")

        flags = cx.sbuf([128, 2], F32, "flags")
        cx.dma("sp", flags.t[:], flags_d, [], [flags])
        cx.dma("sp", convw.t[:], convw_d, [], [convw])
        cx.dma("sp", gain_mix.t[:], gains_d[:, 0:32], [], [gain_mix])
        cx.dma("sp", gain_ffn.t[:], gains_d[:, 32:64], [], [gain_ffn])
        cx.dma("pool", identb.t[:], ident_d, [], [identb])
        cx.op("dve", [], [carry], lambda E: E.memset(carry.t[:], 0.0))

        hidT3 = hidT.t[:].rearrange("p (k n) -> p k n", k=43)

        def wview(wp, k):
            return wp.t[:].rearrange("p (k n) -> p k n", k=k)

        def dense_tok(nsub, wsrc, k0, nk, acts):
            npan = (nk + 7) // 8
            for cb in range(8):
                for kc in range(npan):
                    kk = min(8, nk - kc * 8)
                    wp = wring.next()
                    r0 = (k0 + kc * 8) * 128
                    cx.dma("pool", wview(wp, 8)[:, 0:kk, :],
                           wsrc[r0:r0 + kk * 128, cb * 512:(cb + 1) * 512].rearrange("(k p) n -> p k n", p=128),
                           [], [wp])
                    for s in range(nsub):
                        def mm(E, s=s, kc=kc, kk=kk, wp=wp):
                            r = None
                            for k in range(kk):
                                r = E.matmul(acc[s].t[:], lhsT=acts(kc * 8 + k, s), rhs=wview(wp, 8)[:, k, :],
                                             start=(kc == 0 and k == 0), stop=(kc == npan - 1 and k == kk - 1))
                            return r
                        cx.op("pe", [actT, hidT, wp], [acc[s]], mm)
                for s in range(nsub):
                    cx.op("dve", [acc[s], xs[s]], [xs[s]], lambda E, s=s: E.tensor_tensor(
                        out=xs[s].t[:, cb * 512:(cb + 1) * 512], in0=xs[s].t[:, cb * 512:(cb + 1) * 512],
                        in1=acc[s].t[:], op=ALU.add))

        def do_block(row0, nsub, halo):
            T = nsub * 128
            for s in range(nsub):
                r = row0 + s * 128
                cx.dma("sp", xs[s].t[:], x[r:r + 128, :], [], [xs[s]])
                hb = hbs.next()
                cx.dma("pool", hb.t[:, 0:2048], mix[r:r + 128, 0:2048], [], [hb])
                cx.dma("pool", hb.t[:, 2048:4096], mix[r:r + 128, 2048:4096], [], [hb])
                junk = sgs.next()
                ss, ms, sd, rstd = smalls.next(), smalls.next(), smalls.next(), smalls.next()
                cx.op("act", [hb], [junk, ss], lambda E, hb=hb, junk=junk, ss=ss: E.activation(
                    out=junk.t[:, 0:512], in_=hb.t[:, 2048:2560], func=AF.Square, accum_out=ss.t[:, 0:1]))
                cx.op("act", [hb], [junk, ss], lambda E, hb=hb, junk=junk, ss=ss: E.activation(
                    out=junk.t[:, 0:512], in_=hb.t[:, 2560:3072], func=AF.Square, accum_out=ss.t[:, 1:2]))
                cx.op("dve", [ss], [ss], lambda E, ss=ss: E.tensor_tensor(
                    out=ss.t[:, 0:1], in0=ss.t[:, 0:1], in1=ss.t[:, 1:2], op=ALU.add))
                emit_rstd(cx, ss, ms, sd, rstd, 1024)
                cx.op("dve", [hb, rstd], [hb], lambda E, hb=hb, rstd=rstd: E.tensor_scalar(
                    out=hb.t[:, 2048:3072], in0=hb.t[:, 2048:3072], scalar1=rstd.t[:, 0:1], scalar2=None,
                    op0=ALU.mult))
                emit_transposeT(cx, hb, actT, s, gain_mix, identb, tps)
            dense_tok(nsub, w_out, 0, 32, lambda k, s: actT.t[:, k, s * 128:(s + 1) * 128])
            for s in range(nsub):
                hb = hbs.next()
                ss, ms, sd, rstd = smalls.next(), smalls.next(), smalls.next(), smalls.next()
                cx.op("act", [xs[s]], [hb, ss], lambda E, hb=hb, ss=ss, s=s: E.activation(
                    out=hb.t[:], in_=xs[s].t[:], func=AF.Square, accum_out=ss.t[:, 0:1]))
                emit_rstd(cx, ss, ms, sd, rstd, D)
                cx.op("dve", [xs[s], rstd], [hb], lambda E, hb=hb, rstd=rstd, s=s: E.tensor_scalar(
                    out=hb.t[:], in0=xs[s].t[:], scalar1=rstd.t[:, 0:1], scalar2=None, op0=ALU.mult))
                emit_transposeT(cx, hb, actT, s, gain_ffn, identb, tps)
            pi = 0
            for half in range(2):
                for hp in range(43):
                    ht = half * 43 + hp
                    pg = acc[(pi % 2) * 2]
                    pu = acc[(pi % 2) * 2 + 1]
                    pi += 1
                    wg = wring.next()
                    cx.dma("pool", wview(wg, 32), w_gate[:, ht * 128:(ht + 1) * 128].rearrange("(k p) n -> p k n", p=128),
                           [], [wg])

                    def mmg(E, wg=wg, pg=pg):
                        r = None
                        for k in range(32):
                            r = E.matmul(pg.t[:, 0:T], lhsT=wview(wg, 32)[:, k, :], rhs=actT.t[:, k, 0:T],
                                         start=(k == 0), stop=(k == 31))
                        return r
                    cx.op("pe", [actT, wg], [pg], mmg)
                    if halo:
                        cx.op("act", [pg], [carry], lambda E, pg=pg, ht=ht: E.activation(
                            out=carry.t[:, ht * 2:ht * 2 + 2], in_=pg.t[:, T - 2:T], func=AF.Copy))
                        continue
                    wu = wring.next()
                    cx.dma("pool", wview(wu, 32), w_up[:, ht * 128:(ht + 1) * 128].rearrange("(k p) n -> p k n", p=128),
                           [], [wu])

                    def mmu(E, wu=wu, pu=pu):
                        r = None
                        for k in range(32):
                            r = E.matmul(pu.t[:, 0:T], lhsT=wview(wu, 32)[:, k, :], rhs=actT.t[:, k, 0:T],
                                         start=(k == 0), stop=(k == 31))
                        return r
                    cx.op("pe", [actT, wu], [pu], mmu)
                    gb, a1, sg = gbufs.next(), a1s.next(), sgs.next()
                    cx.op("act", [pg], [gb], lambda E, pg=pg, gb=gb: E.activation(
                        out=gb.t[:, 2:2 + T], in_=pg.t[:, 0:T], func=AF.Copy))
                    cx.op("dve", [carry], [gb], lambda E, gb=gb, ht=ht: E.tensor_copy(
                        out=gb.t[:, 0:2], in_=carry.t[:, ht * 2:ht * 2 + 2]))
                    cx.op("dve", [gb], [carry], lambda E, gb=gb, ht=ht: E.tensor_copy(
                        out=carry.t[:, ht * 2:ht * 2 + 2], in_=gb.t[:, T:T + 2]))
                    cw = lambda i, ht=ht: convw.t[:, ht * 4 + i:ht * 4 + i + 1]
                    cx.op("dve", [gb, convw], [a1], lambda E, gb=gb, a1=a1, cw=cw: E.tensor_scalar(
                        out=a1.t[:, 0:T], in0=gb.t[:, 2:2 + T], scalar1=cw(2), scalar2=cw(3),
                        op0=ALU.mult, op1=ALU.add))
                    cx.op("dve", [gb, convw, a1], [a1], lambda E, gb=gb, a1=a1, cw=cw: E.scalar_tensor_tensor(
                        out=a1.t[:, 0:T], in0=gb.t[:, 1:1 + T], scalar=cw(1), in1=a1.t[:, 0:T],
                        op0=ALU.mult, op1=ALU.add))
                    cx.op("dve", [gb, convw, a1], [a1], lambda E, gb=gb, a1=a1, cw=cw: E.scalar_tensor_tensor(
                        out=a1.t[:, 0:T], in0=gb.t[:, 0:T], scalar=cw(0), in1=a1.t[:, 0:T],
                        op0=ALU.mult, op1=ALU.add))
                    cx.op("act", [a1], [sg], lambda E, a1=a1, sg=sg: E.activation(
                        out=sg.t[:, 0:T], in_=a1.t[:, 0:T], func=AF.Silu))
                    cx.op("dve", [sg, pu], [hidT], lambda E, sg=sg, pu=pu, hp=hp: E.tensor_tensor(
                        out=hidT3[:, hp, 0:T], in0=sg.t[:, 0:T], in1=pu.t[:, 0:T], op=ALU.mult))
                if halo:
                    continue
                dense_tok(nsub, w_down, half * 43, 43, lambda k, s: hidT3[:, k, s * 128:(s + 1) * 128])
            if halo:
                return
            fbc = hidT.t[:].bitcast(F32)[:, 0:D]
            cx.dma("sp", fbc, fnorm_d.partition_broadcast(128), [], [hidT])
            for s in range(nsub):
                r = row0 - 128 + s * 128
                hb = hbs.next()
                ss, ms, sd, rstd = smalls.next(), smalls.next(), smalls.next(), smalls.next()
                cx.op("act", [xs[s]], [hb, ss], lambda E, hb=hb, ss=ss, s=s: E.activation(
                    out=hb.t[:], in_=xs[s].t[:], func=AF.Square, accum_out=ss.t[:, 0:1]))
                emit_rstd(cx, ss, ms, sd, rstd, D)
                cx.op("dve", [rstd, flags], [rstd], lambda E, rstd=rstd: E.tensor_scalar(
                    out=rstd.t[:, 0:1], in0=rstd.t[:, 0:1], scalar1=flags.t[:, 1:2], scalar2=flags.t[:, 0:1],
                    op0=ALU.mult, op1=ALU.add))
                cx.op("dve", [xs[s], rstd, hidT], [xs[s]], lambda E, rstd=rstd, s=s, fbc=fbc: E.scalar_tensor_tensor(
                    out=xs[s].t[:], in0=xs[s].t[:], scalar=rstd.t[:, 0:1], in1=fbc, op0=ALU.mult, op1=ALU.mult))
                cx.dma("sp", y[r:r + 128, :], xs[s].t[:], [xs[s], outbuf], [])

        do_block(0, 1, True)
        ntile = ntok // 512
        for ti in range(ntile):
            do_block(128 + ti * 512, 4, False)
        cx.finish([outbuf])
    return nc


def _colT(v):
    return np.ascontiguousarray(v.reshape(-1, 128).T)


def k2_consts(inp, l, final):
    gm = np.concatenate([np.ones(2048, np.float32), inp["swa_norm"][l].astype(np.float32), np.ones(1024, np.float32)])
    gains = np.concatenate([_colT(gm), _colT(inp["ffn_norm"][l])], axis=1).astype(np.float32)
    cw = np.concatenate([inp["ffn_conv_w"][l], inp["ffn_conv_b"][l][None, :]], axis=0)
    convw = np.ascontiguousarray(cw.reshape(4, NHT, 128).transpose(2, 1, 0).reshape(128, NHT * 4)).astype(np.float32)
    return {
        "w_out": np.ascontiguousarray(inp["w_out"][l]), "w_gate": np.ascontiguousarray(inp["w_gate"][l]),
        "w_up": np.ascontiguousarray(inp["w_up"][l]), "w_down": np.ascontiguousarray(inp["w_down"][l]),
        "gains": np.ascontiguousarray(gains), "convw": convw, "ident": np.eye(128, dtype=np.float32),
        "fnorm": (np.ascontiguousarray(inp["final_norm"]).astype(np.float32) if final else np.ones(D, np.float32)),
        "flags": np.ascontiguousarray(np.tile(np.array([[0.0, 1.0]] if final else [[1.0, 0.0]], np.float32), (128, 1))),
    }


def k2_shard(xflat, mixflat, c, ntok, seq):
    r0 = c * ntok
    xs = np.zeros((128 + ntok, D), np.float32)
    ms = np.zeros((128 + ntok, D), np.float32)
    xs[128:] = xflat[r0:r0 + ntok]
    ms[128:] = mixflat[r0:r0 + ntok]
    if r0 % seq != 0:
        xs[:128] = xflat[r0 - 128:r0]
        ms[:128] = mixflat[r0 - 128:r0]
    return xs, ms


NF = 14
TB = [(NF * 128, 512), (NF * 128 + 512, 456), (NF * 128 + 968, 256)]
NCOL1 = NF * 128 + 1224
C_ID, C_U, C_LST, C_ONE, C_BU, C_BL = 0, 128, 256, 384, 512, 640
C_GAIN, C_CONV, C_DTB, C_ALOG, C_DSK, C_SNW, C_SINK, C_BG, C_GNW = 768, 800, 840, 848, 856, 1368, 1880, 1884, 2012
CW1 = 2268


def build_k1(S):
    nsup = S // 512
    nc = bass.Bass("TRN2", target_bir_lowering=False)
    x = nc.dram_tensor("x", [S, D], F32, kind="ExternalInput").ap()
    w = nc.dram_tensor("w", [D, NCOL1], F32, kind="ExternalInput").ap()
    cst_d = nc.dram_tensor("cst", [128, CW1], F32, kind="ExternalInput").ap()
    bias_d = nc.dram_tensor("bias", [128, 2048], F32, kind="ExternalInput").ap()
    wgl_d = nc.dram_tensor("wgl", [16, 128], F32, kind="ExternalInput").ap()
    out = nc.dram_tensor("o", [S, 1024], F32, kind="ExternalOutput").ap()

    with ExitStack() as es:
        cx = Ctx(nc, es)
        sb = cx.sbuf
        xin = sb([128, D], F32, "xin")
        hb = sb([128, D], BF16, "hb")
        hT = sb([128, 32, 512], BF16, "hT")
        wring = Ring([sb([128, 4096], BF16, "wp%d" % i) for i in range(3)])
        convb = [sb([128, 515], F32, "cvb%d" % i) for i in range(8)]
        xcT = [sb([128, 512], BF16, "xcT%d" % i) for i in range(4)]
        BT = [sb([128, 512], BF16, "BT%d" % i) for i in range(2)]
        CT = [sb([128, 512], BF16, "CT%d" % i) for i in range(2)]
        qT = [sb([128, 512], BF16, "qT%d" % i) for i in range(4)]
        kkT = sb([128, 640], BF16, "kkT")
        gqT = sb([128, 512], F32, "gqT")
        gkT = sb([128, 512], F32, "gkT")
        lrT = sb([128, 512], F32, "lrT")
        tokb = [[sb([128, n], F32, "tok%d_%d" % (s, i)) for i, (_, n) in enumerate(TB)] for s in range(4)]
        cst = sb([128, CW1], F32, "cst")
        wgl = sb([16, 128], F32, "wgl")
        mbP = sb([128, 512], F32, "mbP")
        mbO = sb([128, 512], F32, "mbO")
        identb = sb([128, 128], BF16, "identb")
        ostage = Ring([sb([128, 1024], F32, "ost%d" % i) for i in range(2)])
        smalls = Ring([sb([128, 8], F32, "sm%d" % i) for i in range(24)])
        negA = sb([128, 8], F32, "negA")
        esink = sb([128, 4], F32, "esink")
        Ua = sb([128, 512], F32, "Ua")
        Ebuf = sb([128, 512], F32, "Ebuf")
        cbm = sb([128, 128], F32, "cbm")
        MT = sb([128, 512], BF16, "MT")
        xd = sb([128, 256], BF16, "xd")
        xD = sb([128, 256], F32, "xD")
        Btok = sb([128, 128], BF16, "Btok")
        xdd = sb([128, 256], BF16, "xdd")
        ty = sb([128, 256], F32, "ty")
        sz = sb([128, 256], F32, "sz")
        junk = sb([128, 256], F32, "junk")
        state = [sb([128, 256], F32, "st%d" % i) for i in range(2)]
        stateb = [sb([128, 256], BF16, "stb%d" % i) for i in range(2)]
        v1 = [sb([128, 65], BF16, "v1_%d" % i) for i in range(5)]
        tP = sb([128, 512], F32, "tP")
        pP = sb([128, 512], BF16, "pP")
        pO = sb([128, 512], BF16, "pO")
        xg = sb([128, 128], F32, "xg")
        la = sb([128, 128], F32, "la")
        edecT = sb([128, 128], F32, "edecT")
        einvT = sb([128, 128], F32, "einvT")
        qdA = sb([128, 128], BF16, "qdA")
        qdB = sb([128, 128], BF16, "qdB")
        kinvT = sb([128, 128], BF16, "kinvT")
        erem = sb([128, 128], F32, "erem")
        kend = sb([128, 128], BF16, "kend")
        kend1 = sb([128, 128], BF16, "kend1")
        vb = sb([128, 256], BF16, "vb")
        attm = sb([128, 128], BF16, "attm")
        SA = sb([128, 256], F32, "SA")
        SB = sb([128, 256], F32, "SB")
        S0b = sb([128, 256], BF16, "S0b")
        S1b = sb([128, 256], BF16, "S1b")
        on = sb([128, 256], F32, "on")
        sg = sb([128, 256], F32, "sg")
        outbuf = sb([1, 4], F32, "outdummy")

        pr = Ring([cx.psum([128, 512], F32, "pr%d" % i) for i in range(6)])
        tps = Ring([cx.psum([128, 1024], BF16, "tp%d" % i) for i in range(2)])

        def C(c0, n):
            return cst.t[:, c0:c0 + n]

        def b3(ap, a, b):
            return ap.unsqueeze(2).broadcast_to([128, a, b])

        def v3(ap, a):
            return ap.rearrange("p (a b) -> p a b", a=a)

        cx.dma("sp", cst.t[:], cst_d, [], [cst])
        cx.dma("sp", wgl.t[:], wgl_d, [], [wgl])
        cx.dma("pool", identb.t[:], cst_d[:, C_ID:C_ID + 128], [], [identb])
        cx.dma("sp", mbP.t[:], bias_d[:, 0:512], [], [mbP])
        cx.dma("sp", mbO.t[:], bias_d[:, 512:1024], [], [mbO])
        cx.dma("sp", tP.t[:], bias_d[:, 1024:1536], [], [tP])
        cx.op("dve", [mbP, tP], [mbP], lambda E: E.tensor_tensor(out=mbP.t[:], in0=mbP.t[:], in1=tP.t[:], op=ALU.add))
        cx.dma("sp", tP.t[:], bias_d[:, 1536:2048], [], [tP])
        cx.op("dve", [mbO, tP], [mbO], lambda E: E.tensor_tensor(out=mbO.t[:], in0=mbO.t[:], in1=tP.t[:], op=ALU.add))
        cx.op("act", [cst], [negA], lambda E: E.activation(out=negA.t[:], in_=C(C_ALOG, 8), func=AF.Exp))
        cx.op("dve", [negA], [negA], lambda E: E.tensor_scalar(out=negA.t[:], in0=negA.t[:], scalar1=-1.0, scalar2=None,
                                                              op0=ALU.mult))
        cx.op("act", [cst], [esink], lambda E: E.activation(out=esink.t[:], in_=C(C_SINK, 4), func=AF.Exp))
        for b_ in convb:
            cx.op("dve", [], [b_], lambda E, b_=b_: E.memset(b_.t[:, 0:3], 0.0))
        for b_ in state + [SA, SB]:
            cx.op("dve", [], [b_], lambda E, b_=b_: E.memset(b_.t[:], 0.0))
        for b_ in stateb + [S0b, S1b, qdA, qdB, kkT, kend, kend1] + qT:
            cx.op("dve", [], [b_], lambda E, b_=b_: E.memset(b_.t[:], 0.0))
        for b_ in v1:
            cx.op("dve", [], [b_], lambda E, b_=b_: E.memset(b_.t[:], 1.0))

        def wview(wp, k):
            return wp.t[:].rearrange("p (k n) -> p k n", k=k)

        def rstd_of(ss, dim):
            ms, sd, rstd = smalls.next(), smalls.next(), smalls.next()
            emit_rstd(cx, ss, ms, sd, rstd, dim)
            return rstd

        def ssd_chunk(c, ost):
            cs = slice(c * 128, (c + 1) * 128)
            t1 = tokb[c][1]
            tmpa, tmpb, dt8, a8 = smalls.next(), smalls.next(), smalls.next(), smalls.next()
            cx.op("dve", [t1, cst], [tmpa], lambda E: E.tensor_tensor(out=tmpa.t[:, 0:8], in0=t1.t[:, 0:8],
                                                                    in1=C(C_DTB, 8), op=ALU.add))
            cx.op("act", [tmpa], [tmpb], lambda E: E.activation(out=tmpb.t[:, 0:8], in_=tmpa.t[:, 0:8], func=AF.Exp))
            cx.op("act", [tmpb], [dt8], lambda E: E.activation(out=dt8.t[:, 0:8], in_=tmpb.t[:, 0:8], func=AF.Ln,
                                                              bias=1.0, scale=1.0))
            cx.op("dve", [dt8, negA], [a8], lambda E: E.tensor_tensor(out=a8.t[:, 0:8], in0=dt8.t[:, 0:8],
                                                                     in1=negA.t[:, 0:8], op=ALU.mult))
            for g in range(2):
                g4 = slice(g * 4, g * 4 + 4)
                gc = slice(g * 256, (g + 1) * 256)
                cx.op("dve", [cst, a8], [Ua], lambda E: E.tensor_tensor(
                    out=v3(Ua.t[:], 4), in0=C(C_U, 128).unsqueeze(1).broadcast_to([128, 4, 128]),
                    in1=b3(a8.t[:, g4], 4, 128), op=ALU.mult))
                p1 = pr.next()
                cx.op("pe", [cst, Ua], [p1], lambda E: E.matmul(p1.t[:], lhsT=C(C_LST, 128), rhs=Ua.t[:], start=True, stop=True))
                cx.op("act", [p1], [Ebuf], lambda E: E.activation(out=Ebuf.t[:], in_=p1.t[:], func=AF.Exp))
                p2 = pr.next()
                cx.op("pe", [cst, a8], [p2], lambda E: [
                    E.matmul(p2.t[:, 0:4], lhsT=C(C_U, 128), rhs=a8.t[:, g4], start=True, stop=True),
                    E.matmul(p2.t[:, 4:8], lhsT=C(C_ONE, 128), rhs=a8.t[:, g4], start=True, stop=True)])
                ea8 = smalls.next()
                cx.op("act", [p2], [ea8], lambda E: E.activation(out=ea8.t[:, 0:8], in_=p2.t[:, 0:8], func=AF.Exp))
                p3 = pr.next()
                cx.op("pe", [BT[g], CT[g]], [p3], lambda E: E.matmul(p3.t[:, 0:128], lhsT=BT[g].t[:, cs], rhs=CT[g].t[:, cs],
                                                                    start=True, stop=True))
                cx.op("dve", [p3, cst], [cbm], lambda E: E.tensor_tensor(out=cbm.t[:], in0=p3.t[:, 0:128], in1=C(C_U, 128),
                                                                        op=ALU.mult))
                cx.op("dve", [Ebuf, cbm], [MT], lambda E: E.tensor_tensor(
                    out=v3(MT.t[:], 4), in0=v3(Ebuf.t[:], 4), in1=cbm.t[:].unsqueeze(1).broadcast_to([128, 4, 128]),
                    op=ALU.mult))
                tp = tps.next()
                cx.op("pe", [xcT[2 * g], xcT[2 * g + 1], BT[g], identb], [tp], lambda E: [
                    E.transpose(out=tp.t[:, 0:128], in_=xcT[2 * g].t[:, cs], identity=identb.t[:]),
                    E.transpose(out=tp.t[:, 128:256], in_=xcT[2 * g + 1].t[:, cs], identity=identb.t[:]),
                    E.transpose(out=tp.t[:, 256:384], in_=BT[g].t[:, cs], identity=identb.t[:])])
                cx.op("dve", [tp, dt8], [xd], lambda E: E.tensor_tensor(
                    out=v3(xd.t[:], 4), in0=v3(tp.t[:, 0:256], 4), in1=b3(dt8.t[:, g4], 4, 64), op=ALU.mult))
                cx.op("dve", [tp, cst], [xD], lambda E: E.tensor_tensor(
                    out=xD.t[:], in0=tp.t[:, 0:256], in1=C(C_DSK + g * 256, 256), op=ALU.mult))
                cx.op("act", [tp], [Btok], lambda E: E.activation(out=Btok.t[:], in_=tp.t[:, 256:384], func=AF.Copy))
                cx.op("dve", [xd, Ebuf], [xdd], lambda E: E.tensor_tensor(
                    out=v3(xdd.t[:], 4), in0=v3(xd.t[:], 4), in1=b3(v3(Ebuf.t[:], 4)[:, :, 127], 4, 64), op=ALU.mult))
                p4 = pr.next()
                cx.op("pe", [MT, xd], [p4], lambda E: [
                    E.matmul(p4.t[:, h * 64:(h + 1) * 64], lhsT=MT.t[:, h * 128:(h + 1) * 128],
                             rhs=xd.t[:, h * 64:(h + 1) * 64], start=True, stop=True) for h in range(4)])
                p5 = pr.next()
                cx.op("pe", [CT[g], stateb[g]], [p5], lambda E: E.matmul(
                    p5.t[:, 0:256], lhsT=CT[g].t[:, cs], rhs=stateb[g].t[:], start=True, stop=True))
                cx.op("dve", [p5, ea8], [ty], lambda E: E.tensor_tensor(
                    out=v3(ty.t[:], 4), in0=v3(p5.t[:, 0:256], 4), in1=b3(ea8.t[:, 0:4], 4, 64), op=ALU.mult))
                cx.op("dve", [ty, p4], [ty], lambda E: E.tensor_tensor(out=ty.t[:], in0=ty.t[:], in1=p4.t[:, 0:256], op=ALU.add))
                cx.op("dve", [ty, xD], [ty], lambda E: E.tensor_tensor(out=ty.t[:], in0=ty.t[:], in1=xD.t[:], op=ALU.add))
                z = tokb[c][0]
                cx.op("act", [z], [sz], lambda E: E.activation(out=sz.t[:], in_=z.t[:, gc], func=AF.Silu))
                cx.op("dve", [ty, sz], [ty], lambda E: E.tensor_tensor(out=ty.t[:], in0=ty.t[:], in1=sz.t[:], op=ALU.mult))
                ss = smalls.next()
                cx.op("act", [ty], [junk, ss], lambda E: E.activation(out=junk.t[:], in_=ty.t[:], func=AF.Square,
                                                                     accum_out=ss.t[:, 0:1]))
                rstd = rstd_of(ss, 256)
                cx.op("dve", [ty, rstd, cst], [ost], lambda E: E.scalar_tensor_tensor(
                    out=ost.t[:, gc], in0=ty.t[:], scalar=rstd.t[:, 0:1], in1=C(C_SNW + g * 256, 256),
                    op0=ALU.mult, op1=ALU.mult))
                p6 = pr.next()
                cx.op("pe", [Btok, xdd], [p6], lambda E: E.matmul(p6.t[:, 0:256], lhsT=Btok.t[:], rhs=xdd.t[:],
                                                                 start=True, stop=True))
                cx.op("dve", [state[g], ea8], [state[g]], lambda E: E.tensor_tensor(
                    out=v3(state[g].t[:], 4), in0=v3(state[g].t[:], 4), in1=b3(ea8.t[:, 4:8], 4, 64), op=ALU.mult))
                cx.op("dve", [state[g], p6], [state[g]], lambda E: E.tensor_tensor(
                    out=state[g].t[:], in0=state[g].t[:], in1=p6.t[:, 0:256], op=ALU.add))
                cx.op("act", [state[g]], [stateb[g]], lambda E: E.activation(out=stateb[g].t[:], in_=state[g].t[:],
                                                                            func=AF.Copy))

        def swa_chunk(c, first, ost):
            cs = slice(c * 128, (c + 1) * 128)
            prev = slice(c * 128, (c + 1) * 128)
            own = slice(128 + c * 128, 256 + c * 128)
            t1 = tokb[c][1]
            cx.op("act", [t1], [v1[c + 1]], lambda E: E.activation(out=v1[c + 1].t[:, 0:64], in_=t1.t[:, 8:72], func=AF.Copy))
            pv = pr.next()
            halves = ([] if first else [(prev, mbP, pP, v1[c])]) + [(own, mbO, pO, v1[c + 1])]
            for (ks, mb, pp, _) in halves:
                ps = pr.next()

                def mm(E, ps=ps, ks=ks):
                    r = None
                    for h in range(4):
                        r = E.matmul(ps.t[:, h * 128:(h + 1) * 128], lhsT=kkT.t[:, ks], rhs=qT[h].t[:, cs],
                                     start=True, stop=True)
                    return r
                cx.op("pe", [kkT] + qT, [ps], mm)
                cx.op("dve", [ps, mb], [tP], lambda E, ps=ps, mb=mb: E.scalar_tensor_tensor(
                    out=tP.t[:], in0=ps.t[:], scalar=0.125, in1=mb.t[:], op0=ALU.mult, op1=ALU.add))
                cx.op("act", [tP], [pp], lambda E, pp=pp: E.activation(out=pp.t[:], in_=tP.t[:], func=AF.Exp))

            def mo(E):
                r = None
                for h in range(4):
                    for i, (_, _, pp, vv) in enumerate(halves):
                        r = E.matmul(pv.t[:, h * 65:(h + 1) * 65], lhsT=pp.t[:, h * 128:(h + 1) * 128], rhs=vv.t[:],
                                     start=(i == 0), stop=(i == len(halves) - 1))
                return r
            cx.op("pe", [pP, pO, v1[c], v1[c + 1]], [pv], mo)
            den, rden = smalls.next(), smalls.next()
            pv3 = pv.t[:, 0:260].rearrange("p (h d) -> p h d", h=4)
            cx.op("dve", [pv, esink], [den], lambda E: E.tensor_tensor(out=den.t[:, 0:4], in0=pv3[:, :, 64], in1=esink.t[:],
                                                                      op=ALU.add))
            cx.op("dve", [den], [rden], lambda E: E.reciprocal(out=rden.t[:, 0:4], in_=den.t[:, 0:4]))
            cx.op("dve", [pv, rden], [ost], lambda E: E.tensor_tensor(
                out=v3(ost.t[:, 512:768], 4), in0=pv3[:, :, 0:64], in1=b3(rden.t[:, 0:4], 4, 64), op=ALU.mult))

        def gla_chunk(c, ost):
            cs = slice(c * 128, (c + 1) * 128)
            t1 = tokb[c][1]
            gk_tok = t1.t[:, 72:200]
            gv_tok = t1.t[:, 200:456]
            g_tok = tokb[c][2].t[:, 0:256]
            p1 = pr.next()
            cx.op("pe", [lrT, wgl], [p1], lambda E: E.matmul(p1.t[:, 0:128], lhsT=lrT.t[0:16, cs], rhs=wgl.t[:],
                                                            start=True, stop=True))
            cx.op("dve", [p1, cst], [xg], lambda E: E.tensor_tensor(out=xg.t[:], in0=p1.t[:, 0:128], in1=C(C_BG, 128), op=ALU.add))
            cx.op("act", [xg], [xg], lambda E: E.activation(out=xg.t[:], in_=xg.t[:], func=AF.Exp, scale=-1.0))
            cx.op("act", [xg], [xg], lambda E: E.activation(out=xg.t[:], in_=xg.t[:], func=AF.Ln, bias=1.0, scale=1.0))
            cx.op("dve", [xg], [la], lambda E: E.tensor_scalar(out=la.t[:], in0=xg.t[:], scalar1=-1.0 / 16.0, scalar2=None,
                                                              op0=ALU.mult))
            p2 = pr.next()
            cx.op("pe", [la, cst], [p2], lambda E: E.matmul(p2.t[:, 0:128], lhsT=la.t[:], rhs=C(C_BU, 128), start=True, stop=True))
            p3 = pr.next()
            cx.op("pe", [la, cst], [p3], lambda E: E.matmul(p3.t[:, 0:128], lhsT=C(C_BL, 128), rhs=la.t[:], start=True, stop=True))
            cx.op("act", [p2], [edecT], lambda E: E.activation(out=edecT.t[:], in_=p2.t[:, 0:128], func=AF.Exp))
            cx.op("act", [p2], [einvT], lambda E: E.activation(out=einvT.t[:], in_=p2.t[:, 0:128], func=AF.Exp, scale=-1.0))
            cx.op("act", [p3], [erem], lambda E: E.activation(out=erem.t[:], in_=p3.t[:, 0:128], func=AF.Exp))
            sc = 128.0 ** -0.5
            cx.op("dve", [gqT, edecT], [qdA], lambda E: E.scalar_tensor_tensor(
                out=qdA.t[:, 0:64], in0=gqT.t[:, c * 128:c * 128 + 64], scalar=sc, in1=edecT.t[:, 0:64],
                op0=ALU.mult, op1=ALU.mult))
            cx.op("dve", [gqT, edecT], [qdB], lambda E: E.scalar_tensor_tensor(
                out=qdB.t[:, 64:128], in0=gqT.t[:, c * 128 + 64:c * 128 + 128], scalar=sc, in1=edecT.t[:, 64:128],
                op0=ALU.mult, op1=ALU.mult))
            cx.op("dve", [gkT, einvT], [kinvT], lambda E: E.tensor_tensor(out=kinvT.t[:], in0=gkT.t[:, cs], in1=einvT.t[:],
                                                                         op=ALU.mult))
            cx.op("dve", [t1, erem], [kend], lambda E: E.tensor_tensor(out=kend.t[0:64, :], in0=t1.t[0:64, 72:200],
                                                                      in1=erem.t[0:64, :], op=ALU.mult))
            cx.op("dve", [t1, erem], [kend1], lambda E: E.tensor_tensor(out=kend1.t[64:128, :], in0=t1.t[64:128, 72:200],
                                                                       in1=erem.t[64:128, :], op=ALU.mult))
            cx.op("act", [t1], [vb], lambda E: E.activation(out=vb.t[:], in_=gv_tok, func=AF.Copy))
            p4 = pr.next()
            cx.op("pe", [kinvT, qdA, qdB], [p4], lambda E: [
                E.matmul(p4.t[:, 0:64], lhsT=kinvT.t[:], rhs=qdA.t[:, 0:64], start=True, stop=True),
                E.matmul(p4.t[:, 64:128], lhsT=kinvT.t[:], rhs=qdB.t[:, 64:128], start=True, stop=True)])
            cx.op("dve", [p4, cst], [attm], lambda E: E.tensor_tensor(out=attm.t[:], in0=p4.t[:, 0:128], in1=C(C_BU, 128),
                                                                     op=ALU.mult))
            p5 = pr.next()
            cx.op("pe", [kend, vb], [p5], lambda E: E.matmul(p5.t[:, 0:256], lhsT=kend.t[:], rhs=vb.t[:],
                                                            start=True, stop=True))
            cx.op("dve", [SA, edecT, p5], [SB], lambda E: E.scalar_tensor_tensor(
                out=SB.t[:], in0=SA.t[:], scalar=edecT.t[:, 63:64], in1=p5.t[:, 0:256], op0=ALU.mult, op1=ALU.add))
            cx.op("act", [SB], [S1b], lambda E: E.activation(out=S1b.t[:], in_=SB.t[:], func=AF.Copy))
            p6 = pr.next()
            cx.op("pe", [attm, vb, qdA, qdB, S0b, S1b], [p6], lambda E: [
                E.matmul(p6.t[:, 0:256], lhsT=attm.t[:], rhs=vb.t[:], start=True, stop=False),
                E.matmul(p6.t[:, 0:256], lhsT=qdA.t[:], rhs=S0b.t[:], start=False, stop=False),
                E.matmul(p6.t[:, 0:256], lhsT=qdB.t[:], rhs=S1b.t[:], start=False, stop=True)])
            p7 = pr.next()
            cx.op("pe", [kend1, vb], [p7], lambda E: E.matmul(p7.t[:, 0:256], lhsT=kend1.t[:], rhs=vb.t[:],
                                                            start=True, stop=True))
            cx.op("dve", [SB, edecT, p7], [SA], lambda E: E.scalar_tensor_tensor(
                out=SA.t[:], in0=SB.t[:], scalar=edecT.t[:, 127:128], in1=p7.t[:, 0:256], op0=ALU.mult, op1=ALU.add))
            cx.op("act", [SA], [S0b], lambda E: E.activation(out=S0b.t[:], in_=SA.t[:], func=AF.Copy))
            ss = smalls.next()
            cx.op("act", [p6], [junk, ss], lambda E: E.activation(out=junk.t[:], in_=p6.t[:, 0:256], func=AF.Square,
                                                                 accum_out=ss.t[:, 0:1]))
            rstd = rstd_of(ss, 256)
            cx.op("dve", [p6, rstd, cst], [on], lambda E: E.scalar_tensor_tensor(
                out=on.t[:], in0=p6.t[:, 0:256], scalar=rstd.t[:, 0:1], in1=C(C_GNW, 256), op0=ALU.mult, op1=ALU.mult))
            cx.op("act", [tokb[c][2]], [sg], lambda E: E.activation(out=sg.t[:], in_=g_tok, func=AF.Silu))
            cx.op("dve", [on, sg], [ost], lambda E: E.tensor_tensor(out=ost.t[:, 768:1024], in0=on.t[:], in1=sg.t[:], op=ALU.mult))

        for st in range(nsup):
            r0 = st * 512
            for s in range(4):
                cx.dma("sp", xin.t[:], x[r0 + s * 128:r0 + (s + 1) * 128, :], [], [xin])
                ss = smalls.next()
                cx.op("act", [xin], [hb, ss], lambda E, ss=ss: E.activation(out=hb.t[:], in_=xin.t[:], func=AF.Square,
                                                                           accum_out=ss.t[:, 0:1]))
                rstd = rstd_of(ss, D)
                cx.op("dve", [xin, rstd], [hb], lambda E, rstd=rstd: E.tensor_scalar(
                    out=hb.t[:], in0=xin.t[:], scalar1=rstd.t[:, 0:1], scalar2=None, op0=ALU.mult))
                gain = Buf(None, "g")
                emit_transposeT(cx, hb, hT, s, _CstView(cst, C_GAIN), identb, tps)
            for f in range(NF):
                wp = wring.next()
                cx.dma("pool", wview(wp, 32), w[:, f * 128:(f + 1) * 128].rearrange("(k p) n -> p k n", p=128), [], [wp])
                ps = pr.next()

                def mmf(E, wp=wp, ps=ps):
                    r = None
                    for k in range(32):
                        r = E.matmul(ps.t[:], lhsT=wview(wp, 32)[:, k, :], rhs=hT.t[:, k, :], start=(k == 0), stop=(k == 31))
                    return r
                cx.op("pe", [hT, wp], [ps], mmf)
                if f < 8:
                    dst, dap = convb[f], convb[f].t[:, 3:515]
                elif f < 10:
                    for e in range(2):
                        qd = qT[(f - 8) * 2 + e]
                        cx.op("act", [ps], [qd], lambda E, ps=ps, qd=qd, e=e: E.activation(
                            out=qd.t[e * 64:(e + 1) * 64, :], in_=ps.t[e * 64:(e + 1) * 64, :], func=AF.Copy))
                    continue
                elif f == 10:
                    dst, dap = kkT, kkT.t[:, 128:640]
                elif f == 11:
                    dst, dap = gqT, gqT.t[:]
                elif f == 12:
                    dst, dap = gkT, gkT.t[:]
                else:
                    dst, dap = lrT, lrT.t[:]
                cx.op("act", [ps], [dst], lambda E, ps=ps, dap=dap: E.activation(out=dap, in_=ps.t[:], func=AF.Copy))
            for bi, (c0, ncol) in enumerate(TB):
                accs = [pr.next() for _ in range(4)]
                for kc in range(4):
                    wp = wring.next()
                    cx.dma("pool", wview(wp, 8)[:, :, 0:ncol],
                           w[kc * 1024:(kc + 1) * 1024, c0:c0 + ncol].rearrange("(k p) n -> p k n", p=128), [], [wp])
                    for s in range(4):
                        def mmt(E, wp=wp, s=s, kc=kc):
                            r = None
                            for k in range(8):
                                r = E.matmul(accs[s].t[:, 0:ncol], lhsT=hT.t[:, kc * 8 + k, s * 128:(s + 1) * 128],
                                             rhs=wview(wp, 8)[:, k, 0:ncol], start=(kc == 0 and k == 0),
                                             stop=(kc == 3 and k == 7))
                            return r
                        cx.op("pe", [hT, wp], [accs[s]], mmt)
                for s in range(4):
                    cx.op("act", [accs[s]], [tokb[s][bi]], lambda E, s=s: E.activation(
                        out=tokb[s][bi].t[:], in_=accs[s].t[:, 0:ncol], func=AF.Copy))
            for f in range(8):
                cb = convb[f]
                cw = lambda i, f=f: C(C_CONV + f * 5 + i, 1)
                a1 = Ua if f % 2 == 0 else Ebuf
                cx.op("dve", [cb, cst], [a1], lambda E, cb=cb, a1=a1, cw=cw: E.tensor_scalar(
                    out=a1.t[:], in0=cb.t[:, 3:515], scalar1=cw(3), scalar2=cw(4), op0=ALU.mult, op1=ALU.add))
                for i in range(3):
                    cx.op("dve", [cb, cst, a1], [a1], lambda E, cb=cb, a1=a1, cw=cw, i=i: E.scalar_tensor_tensor(
                        out=a1.t[:], in0=cb.t[:, i:i + 512], scalar=cw(i), in1=a1.t[:], op0=ALU.mult, op1=ALU.add))
                dst = xcT[f] if f < 4 else (BT[f - 4] if f < 6 else CT[f - 6])
                cx.op("act", [a1], [dst], lambda E, a1=a1, dst=dst: E.activation(out=dst.t[:], in_=a1.t[:], func=AF.Silu))
                cx.op("dve", [cb], [cb], lambda E, cb=cb: E.tensor_copy(out=cb.t[:, 0:3], in_=cb.t[:, 512:515]))
            for c in range(4):
                ost = ostage.next()
                import os as _os
                _parts = _os.environ.get("K1_PARTS", "ssd,swa,gla").split(",")
                if "ssd" in _parts:
                    ssd_chunk(c, ost)
                if "swa" in _parts:
                    swa_chunk(c, st == 0 and c == 0, ost)
                if "gla" in _parts:
                    gla_chunk(c, ost)
                cx.dma("sp", out[r0 + c * 128:r0 + (c + 1) * 128, :], ost.t[:], [ost, outbuf], [])
            cx.op("dve", [kkT], [kkT], lambda E: E.tensor_copy(out=kkT.t[:, 0:128], in_=kkT.t[:, 512:640]))
            cx.op("dve", [v1[4]], [v1[0]], lambda E: E.tensor_copy(out=v1[0].t[:], in_=v1[4].t[:]))
        cx.finish([outbuf])
    return nc


class _CstView:
    class _T:
        def __init__(self, t, c0):
            self.t, self.c0 = t, c0

        def __getitem__(self, idx):
            p, c = idx
            return self.t[p, c.start + self.c0:c.stop + self.c0]

    def __init__(self, parent, c0):
        self.parent = parent
        self.t = _CstView._T(parent.t, c0)
        self.name = parent.name

    @property
    def w(self):
        return self.parent.w

    @property
    def r(self):
        return self.parent.r


def _t5_bucket_np(dist):
    d = np.maximum(dist.astype(np.float32), np.float32(1.0))
    large = 16 + (np.log(d / np.float32(16)) / np.float32(math.log(128 / 16)) * np.float32(16)).astype(np.int32)
    large = np.minimum(large, 31)
    return np.where(dist < 16, dist, large)


def k1_consts(inp, l, j):
    f32 = np.float32
    G = [2 * j, 2 * j + 1]
    w_in = inp["w_in"][l]
    cols = []
    for g in G:
        cols.append(np.arange(2048 + g * 256, 2048 + (g + 1) * 256))
    for g in G:
        cols.append(np.arange(4096 + g * 128, 4096 + (g + 1) * 128))
    for g in G:
        cols.append(np.arange(5120 + g * 128, 5120 + (g + 1) * 128))
    cols.append(np.arange(6176 + 4 * j * 64, 6176 + (4 * j + 4) * 64))
    kc = np.arange(7200 + j * 64, 7200 + (j + 1) * 64)
    cols += [kc, kc]
    cols.append(np.arange(7712 + j * 128, 7712 + (j + 1) * 128))
    cols.append(np.arange(8224 + j * 128, 8224 + (j + 1) * 128))
    colsF = np.concatenate(cols)
    wF = w_in[:, colsF]
    wlr = np.concatenate([w_in[:, 10784:10800], np.zeros((D, 112), f32)], axis=1)
    colsT = np.concatenate([
        np.arange(G[0] * 256, (G[0] + 1) * 256), np.arange(G[1] * 256, (G[1] + 1) * 256),
        np.arange(6144 + 4 * G[0], 6144 + 4 * G[0] + 4), np.arange(6144 + 4 * G[1], 6144 + 4 * G[1] + 4),
        np.arange(7456 + j * 64, 7456 + (j + 1) * 64),
        np.arange(8224 + j * 128, 8224 + (j + 1) * 128),
        np.arange(8736 + j * 256, 8736 + (j + 1) * 256),
        np.arange(9760 + j * 256, 9760 + (j + 1) * 256)])
    w = np.ascontiguousarray(np.concatenate([wF, wlr, w_in[:, colsT]], axis=1), dtype=f32)
    assert w.shape[1] == NCOL1

    cst = np.zeros((128, CW1), f32)
    ii = np.arange(128)
    cst[:, C_ID:C_ID + 128] = np.eye(128, dtype=f32)
    cst[:, C_U:C_U + 128] = (ii[:, None] <= ii[None, :])
    cst[:, C_LST:C_LST + 128] = (ii[:, None] > ii[None, :])
    cst[:, C_ONE:C_ONE + 128] = 1.0
    same = (ii[:, None] // 64) == (ii[None, :] // 64)
    cst[:, C_BU:C_BU + 128] = same & (ii[:, None] <= ii[None, :])
    cst[:, C_BL:C_BL + 128] = same & (ii[:, None] > ii[None, :])
    cst[:, C_GAIN:C_GAIN + 32] = _colT(inp["attn_norm"][l])
    chs = [G[0] * 256, G[0] * 256 + 128, G[1] * 256, G[1] * 256 + 128,
           2048 + G[0] * 128, 2048 + G[1] * 128, 3072 + G[0] * 128, 3072 + G[1] * 128]
    cwt, cbs = inp["ssd_conv_w"][l], inp["ssd_conv_b"][l]
    for f, ch in enumerate(chs):
        for i in range(4):
            cst[:, C_CONV + f * 5 + i] = cwt[i, ch:ch + 128]
        cst[:, C_CONV + f * 5 + 4] = cbs[ch:ch + 128]
    hs = np.concatenate([np.arange(4 * G[0], 4 * G[0] + 4), np.arange(4 * G[1], 4 * G[1] + 4)])
    cst[:, C_DTB:C_DTB + 8] = inp["ssd_dt_bias"][l][hs][None, :]
    cst[:, C_ALOG:C_ALOG + 8] = inp["ssd_a_log"][l][hs][None, :]
    cst[:, C_DSK:C_DSK + 512] = np.repeat(inp["ssd_d"][l][hs], 64)[None, :]
    cst[:, C_SNW:C_SNW + 512] = np.concatenate([inp["ssd_norm"][l][g * 256:(g + 1) * 256] for g in G])[None, :]
    cst[:, C_SINK:C_SINK + 4] = inp["swa_sinks"][l][4 * j:4 * j + 4][None, :]
    cst[:, C_BG:C_BG + 128] = inp["gla_b_gate"][l][j * 128:(j + 1) * 128][None, :]
    cst[:, C_GNW:C_GNW + 256] = inp["gla_norm"][l][None, :]

    qi = np.arange(128)[None, :]
    bias = np.zeros((128, 2048), f32)
    rb = inp["rel_bias"]
    for half in range(2):
        kj = np.arange(128)[:, None] + 128 * half
        dist = qi + 128 - kj
        valid = (dist >= 0) & (dist < 128)
        bk = _t5_bucket_np(np.clip(dist, 0, 127))
        for h in range(4):
            bias[:, half * 512 + h * 128: half * 512 + (h + 1) * 128] = rb[bk, 4 * j + h]
            bias[:, 1024 + half * 512 + h * 128: 1024 + half * 512 + (h + 1) * 128] = np.where(valid, 0.0, -30000.0)
    wgl = np.ascontiguousarray(inp["gla_w_gate"][l][:, j * 128:(j + 1) * 128], dtype=f32)
    return {"w": w, "cst": cst, "bias": bias, "wgl": wgl}


def k1_scatter(mix_b, o, j):
    mix_b[:, 2 * j * 256:(2 * j + 2) * 256] = o[:, 0:512]
    mix_b[:, 2048 + j * 256:2048 + (j + 1) * 256] = o[:, 512:768]
    mix_b[:, 3072 + j * 256:3072 + (j + 1) * 256] = o[:, 768:1024]


N2 = 8


def kernel(**inputs):
    import sys
    import time
    t0 = time.time()
    inp = {k: np.asarray(v) for k, v in inputs.items()}
    x = np.ascontiguousarray(inp["x"], dtype=np.float32)
    B, S, _ = x.shape
    xflat = x.reshape(B * S, D)
    ntok = (B * S) // N2
    for l in range(2):
        nc1 = build_k1(S)
        consts = [k1_consts(inp, l, j) for j in range(4)]
        in_maps = [{"x": xflat[(c // 4) * S:(c // 4 + 1) * S], **consts[c % 4]} for c in range(8)]
        res = run_bass_kernel_spmd(nc1, in_maps, core_ids=list(range(8)))
        mix = np.empty((B, S, D), np.float32)
        for c in range(8):
            k1_scatter(mix[c // 4], res.results[c]["o"], c % 4)
        del res, in_maps, consts
        print("[kernel] layer %d K1 done %.1fs" % (l, time.time() - t0), file=sys.stderr, flush=True)
        nc2 = build_k2(ntok)
        c2 = k2_consts(inp, l, l == 1)
        mixflat = mix.reshape(B * S, D)
        in_maps = []
        for c in range(N2):
            xs, ms = k2_shard(xflat, mixflat, c, ntok, S)
            in_maps.append({"x": xs, "mix": ms, **c2})
        res = run_bass_kernel_spmd(nc2, in_maps, core_ids=list(range(N2)))
        xflat = np.concatenate([r["y"] for r in res.results], axis=0)
        del res, in_maps, c2, mix, mixflat
        print("[kernel] layer %d K2 done %.1fs" % (l, time.time() - t0), file=sys.stderr, flush=True)
    return xflat.reshape(B, S, D)
```

```python
import math
from contextlib import ExitStack

import numpy as np
import concourse.bass as bass
import concourse.mybir as mybir
from concourse.bass_utils import run_bass_kernel_spmd

F32 = mybir.dt.float32
BF16 = mybir.dt.bfloat16
AF = mybir.ActivationFunctionType
ALU = mybir.AluOpType

D = 4096
DFF = 11008
NHT = DFF // 128
EPS = 1e-6


class Buf:
    __slots__ = ("t", "w", "r", "name")

    def __init__(self, t, name):
        self.t = t
        self.w = None
        self.r = {}
        self.name = name


class Ctx:
    def __init__(self, nc, es, n_dma_sems=24):
        self.nc = nc
        self.es = es
        self.E = {"pe": nc.tensor, "act": nc.scalar, "dve": nc.vector, "pool": nc.gpsimd, "sp": nc.sync}
        self.sems = {}
        self.cnt = {}
        for k in ("pe", "act", "dve", "pool"):
            self.sems[k] = es.enter_context(nc.semaphore("s_" + k))
            self.cnt[k] = 0
        self.ndma = n_dma_sems
        for i in range(n_dma_sems):
            self.sems[("d", i)] = es.enter_context(nc.semaphore("s_d%d" % i))
            self.cnt[("d", i)] = 0
        self.rr = 0
        self.seen = {k: {} for k in self.E}
        self.nbuf = 0

    def sbuf(self, shape, dt, name=None):
        self.nbuf += 1
        name = "sb_" + (name or str(self.nbuf))
        return Buf(self.es.enter_context(self.nc.sbuf_tensor(name, list(shape), dt)), name)

    def psum(self, shape, dt, name=None):
        self.nbuf += 1
        name = "ps_" + (name or str(self.nbuf))
        return Buf(self.es.enter_context(self.nc.psum_tensor(name, list(shape), dt)), name)

    def _wait(self, eng, deps):
        E = self.E[eng]
        seen = self.seen[eng]
        best = {}
        for (k, v) in deps:
            if v > best.get(k, 0):
                best[k] = v
        for k, v in best.items():
            if k == "pe" and eng == "pe":
                continue
            if seen.get(k, 0) >= v:
                continue
            E.wait_ge(self.sems[k], v)
            seen[k] = v

    @staticmethod
    def _deps(reads, writes):
        deps = []
        for b in reads:
            if b.w is not None:
                deps.append(b.w)
        for b in writes:
            if b.w is not None:
                deps.append(b.w)
            deps.extend(b.r.items())
        return deps

    @staticmethod
    def _commit(ev, reads, writes):
        k, v = ev
        for b in reads:
            if b.r.get(k, 0) < v:
                b.r[k] = v
        for b in writes:
            b.w = ev
            b.r = {}

    def op(self, eng, reads, writes, fn):
        self._wait(eng, self._deps(reads, writes))
        ins = fn(self.E[eng])
        if isinstance(ins, (list, tuple)):
            ins = ins[-1]
        self.cnt[eng] += 1
        ins.then_inc(self.sems[eng], 1)
        self._commit((eng, self.cnt[eng]), reads, writes)

    def dma(self, q, out_ap, in_ap, reads, writes):
        i = self.rr
        self.rr = (self.rr + 1) % self.ndma
        k = ("d", i)
        deps = self._deps(reads, writes)
        if self.cnt[k] > 0:
            deps.append((k, self.cnt[k]))
        self._wait(q, deps)
        ins = self.E[q].dma_start(out=out_ap, in_=in_ap)
        self.cnt[k] += 16
        ins.then_inc(self.sems[k], 16)
        self._commit((k, self.cnt[k]), reads, writes)

    def finish(self, bufs):
        deps = []
        for b in bufs:
            if b.w is not None:
                deps.append(b.w)
            deps.extend(b.r.items())
        self._wait("sp", deps)


class Ring:
    def __init__(self, bufs):
        self.bufs = bufs
        self.i = 0

    def next(self):
        b = self.bufs[self.i % len(self.bufs)]
        self.i += 1
        return b


def emit_rstd(cx, ss, ms, sd, rstd, dim):
    cx.op("dve", [ss], [ms], lambda E: E.tensor_scalar(out=ms.t[:, 0:1], in0=ss.t[:, 0:1], scalar1=1.0 / dim,
                                                       scalar2=EPS, op0=ALU.mult, op1=ALU.add))
    cx.op("act", [ms], [sd], lambda E: E.activation(out=sd.t[:, 0:1], in_=ms.t[:, 0:1], func=AF.Sqrt))
    cx.op("dve", [sd], [rstd], lambda E: E.reciprocal(out=rstd.t[:, 0:1], in_=sd.t[:, 0:1]))


def emit_transposeT(cx, hb, actT, s, gain, identb, tps):
    for g in range(4):
        pt = tps.next()

        def tr(E, pt=pt, g=g):
            r = None
            for j in range(8):
                c0 = (g * 8 + j) * 128
                r = E.transpose(out=pt.t[:, j * 128:(j + 1) * 128], in_=hb.t[:, c0:c0 + 128], identity=identb.t[:])
            return r

        cx.op("pe", [hb, identb], [pt], tr)
        cx.op("dve", [pt, gain], [actT], lambda E, pt=pt, g=g: E.tensor_tensor(
            out=actT.t[:, g * 8:(g + 1) * 8, s * 128:(s + 1) * 128],
            in0=pt.t[:].rearrange("p (a b) -> p a b", a=8),
            in1=gain.t[:, g * 8:(g + 1) * 8].unsqueeze(2).broadcast_to([128, 8, 128]),
            op=ALU.mult))


def build_k2(ntok):
    nc = bass.Bass("TRN2", target_bir_lowering=False)
    x = nc.dram_tensor("x", [128 + ntok, D], F32, kind="ExternalInput").ap()
    mix = nc.dram_tensor("mix", [128 + ntok, D], F32, kind="ExternalInput").ap()
    w_out = nc.dram_tensor("w_out", [D, D], F32, kind="ExternalInput").ap()
    w_gate = nc.dram_tensor("w_gate", [D, DFF], F32, kind="ExternalInput").ap()
    w_up = nc.dram_tensor("w_up", [D, DFF], F32, kind="ExternalInput").ap()
    w_down = nc.dram_tensor("w_down", [DFF, D], F32, kind="ExternalInput").ap()
    gains_d = nc.dram_tensor("gains", [128, 64], F32, kind="ExternalInput").ap()
    convw_d = nc.dram_tensor("convw", [128, NHT * 4], F32, kind="ExternalInput").ap()
    ident_d = nc.dram_tensor("ident", [128, 128], F32, kind="ExternalInput").ap()
    fnorm_d = nc.dram_tensor("fnorm", [D], F32, kind="ExternalInput").ap()
    flags_d = nc.dram_tensor("flags", [128, 2], F32, kind="ExternalInput").ap()
    y = nc.dram_tensor("y", [ntok, D], F32, kind="ExternalOutput").ap()

    with ExitStack() as es:
        cx = Ctx(nc, es)
        xs = [cx.sbuf([128, D], F32, "xs%d" % i) for i in range(4)]
        hbs = Ring([cx.sbuf([128, D], BF16, "hb%d" % i) for i in range(2)])
        actT = cx.sbuf([128, 32, 512], BF16, "actT")
        hidT = cx.sbuf([128, 43 * 512], BF16, "hidT")
        wring = Ring([cx.sbuf([128, 4096], BF16, "wp%d" % i) for i in range(3)])
        gbufs = Ring([cx.sbuf([128, 516], F32, "gb%d" % i) for i in range(2)])
        a1s = Ring([cx.sbuf([128, 512], F32, "a1%d" % i) for i in range(2)])
        sgs = Ring([cx.sbuf([128, 512], F32, "sg%d" % i) for i in range(2)])
        carry = cx.sbuf([128, NHT * 2], F32, "carry")
        convw = cx.sbuf([128, NHT * 4], F32, "convw")
        gain_mix = cx.sbuf([128, 32], F32, "gain_mix")
        gain_ffn = cx.sbuf([128, 32], F32, "gain_ffn")
        identb = cx.sbuf([128, 128], BF16, "identb")
        smalls = Ring([cx.sbuf([128, 4], F32, "sm%d" % i) for i in range(16)])
        acc = [cx.psum([128, 512], F32, "acc%d" % i) for i in range(6)]
        tps = Ring([cx.psum([128, 1024], BF16, "tp%d" % i) for i in range(2)])
        outbuf = cx.sbuf([1, 4], F32, "outdummy")

        flags = cx.sbuf([128, 2], F32, "flags")
        cx.dma("sp", flags.t[:], flags_d, [], [flags])
        cx.dma("sp", convw.t[:], convw_d, [], [convw])
        cx.dma("sp", gain_mix.t[:], gains_d[:, 0:32], [], [gain_mix])
        cx.dma("sp", gain_ffn.t[:], gains_d[:, 32:64], [], [gain_ffn])
        cx.dma("pool", identb.t[:], ident_d, [], [identb])
        cx.op("dve", [], [carry], lambda E: E.memset(carry.t[:], 0.0))

        hidT3 = hidT.t[:].rearrange("p (k n) -> p k n", k=43)

        def wview(wp, k):
            return wp.t[:].rearrange("p (k n) -> p k n", k=k)

        def dense_tok(nsub, wsrc, k0, nk, acts):
            npan = (nk + 7) // 8
            for cb in range(8):
                for kc in range(npan):
                    kk = min(8, nk - kc * 8)
                    wp = wring.next()
                    r0 = (k0 + kc * 8) * 128
                    cx.dma("pool", wview(wp, 8)[:, 0:kk, :],
                           wsrc[r0:r0 + kk * 128, cb * 512:(cb + 1) * 512].rearrange("(k p) n -> p k n", p=128),
                           [], [wp])
                    for s in range(nsub):
                        def mm(E, s=s, kc=kc, kk=kk, wp=wp):
                            r = None
                            for k in range(kk):
                                r = E.matmul(acc[s].t[:], lhsT=acts(kc * 8 + k, s), rhs=wview(wp, 8)[:, k, :],
                                             start=(kc == 0 and k == 0), stop=(kc == npan - 1 and k == kk - 1))
                            return r
                        cx.op("pe", [actT, hidT, wp], [acc[s]], mm)
                for s in range(nsub):
                    cx.op("dve", [acc[s], xs[s]], [xs[s]], lambda E, s=s: E.tensor_tensor(
                        out=xs[s].t[:, cb * 512:(cb + 1) * 512], in0=xs[s].t[:, cb * 512:(cb + 1) * 512],
                        in1=acc[s].t[:], op=ALU.add))

        def do_block(row0, nsub, halo):
            T = nsub * 128
            for s in range(nsub):
                r = row0 + s * 128
                cx.dma("sp", xs[s].t[:], x[r:r + 128, :], [], [xs[s]])
                hb = hbs.next()
                cx.dma("pool", hb.t[:, 0:2048], mix[r:r + 128, 0:2048], [], [hb])
                cx.dma("pool", hb.t[:, 2048:4096], mix[r:r + 128, 2048:4096], [], [hb])
                junk = sgs.next()
                ss, ms, sd, rstd = smalls.next(), smalls.next(), smalls.next(), smalls.next()
                cx.op("act", [hb], [junk, ss], lambda E, hb=hb, junk=junk, ss=ss: E.activation(
                    out=junk.t[:, 0:512], in_=hb.t[:, 2048:2560], func=AF.Square, accum_out=ss.t[:, 0:1]))
                cx.op("act", [hb], [junk, ss], lambda E, hb=hb, junk=junk, ss=ss: E.activation(
                    out=junk.t[:, 0:512], in_=hb.t[:, 2560:3072], func=AF.Square, accum_out=ss.t[:, 1:2]))
                cx.op("dve", [ss], [ss], lambda E, ss=ss: E.tensor_tensor(
                    out=ss.t[:, 0:1], in0=ss.t[:, 0:1], in1=ss.t[:, 1:2], op=ALU.add))
                emit_rstd(cx, ss, ms, sd, rstd, 1024)
                cx.op("dve", [hb, rstd], [hb], lambda E, hb=hb, rstd=rstd: E.tensor_scalar(
                    out=hb.t[:, 2048:3072], in0=hb.t[:, 2048:3072], scalar1=rstd.t[:, 0:1], scalar2=None,
                    op0=ALU.mult))
                emit_transposeT(cx, hb, actT, s, gain_mix, identb, tps)
            dense_tok(nsub, w_out, 0, 32, lambda k, s: actT.t[:, k, s * 128:(s + 1) * 128])
            for s in range(nsub):
                hb = hbs.next()
                ss, ms, sd, rstd = smalls.next(), smalls.next(), smalls.next(), smalls.next()
                cx.op("act", [xs[s]], [hb, ss], lambda E, hb=hb, ss=ss, s=s: E.activation(
                    out=hb.t[:], in_=xs[s].t[:], func=AF.Square, accum_out=ss.t[:, 0:1]))
                emit_rstd(cx, ss, ms, sd, rstd, D)
                cx.op("dve", [xs[s], rstd], [hb], lambda E, hb=hb, rstd=rstd, s=s: E.tensor_scalar(
                    out=hb.t[:], in0=xs[s].t[:], scalar1=rstd.t[:, 0:1], scalar2=None, op0=ALU.mult))
                emit_transposeT(cx, hb, actT, s, gain_ffn, identb, tps)
            pi = 0
            for half in range(2):
                for hp in range(43):
                    ht = half * 43 + hp
                    pg = acc[(pi % 2) * 2]
                    pu = acc[(pi % 2) * 2 + 1]
                    pi += 1
                    wg = wring.next()
                    cx.dma("pool", wview(wg, 32), w_gate[:, ht * 128:(ht + 1) * 128].rearrange("(k p) n -> p k n", p=128),
                           [], [wg])

                    def mmg(E, wg=wg, pg=pg):
                        r = None
                        for k in range(32):
                            r = E.matmul(pg.t[:, 0:T], lhsT=wview(wg, 32)[:, k, :], rhs=actT.t[:, k, 0:T],
                                         start=(k == 0), stop=(k == 31))
                        return r
                    cx.op("pe", [actT, wg], [pg], mmg)
                    if halo:
                        cx.op("act", [pg], [carry], lambda E, pg=pg, ht=ht: E.activation(
                            out=carry.t[:, ht * 2:ht * 2 + 2], in_=pg.t[:, T - 2:T], func=AF.Copy))
                        continue
                    wu = wring.next()
                    cx.dma("pool", wview(wu, 32), w_up[:, ht * 128:(ht + 1) * 128].rearrange("(k p) n -> p k n", p=128),
                           [], [wu])

                    def mmu(E, wu=wu, pu=pu):
                        r = None
                        for k in range(32):
                            r = E.matmul(pu.t[:, 0:T], lhsT=wview(wu, 32)[:, k, :], rhs=actT.t[:, k, 0:T],
                                         start=(k == 0), stop=(k == 31))
                        return r
                    cx.op("pe", [actT, wu], [pu], mmu)
                    gb, a1, sg = gbufs.next(), a1s.next(), sgs.next()
                    cx.op("act", [pg], [gb], lambda E, pg=pg, gb=gb: E.activation(
                        out=gb.t[:, 2:2 + T], in_=pg.t[:, 0:T], func=AF.Copy))
                    cx.op("dve", [carry], [gb], lambda E, gb=gb, ht=ht: E.tensor_copy(
                        out=gb.t[:, 0:2], in_=carry.t[:, ht * 2:ht * 2 + 2]))
                    cx.op("dve", [gb], [carry], lambda E, gb=gb, ht=ht: E.tensor_copy(
                        out=carry.t[:, ht * 2:ht * 2 + 2], in_=gb.t[:, T:T + 2]))
                    cw = lambda i, ht=ht: convw.t[:, ht * 4 + i:ht * 4 + i + 1]
                    cx.op("dve", [gb, convw], [a1], lambda E, gb=gb, a1=a1, cw=cw: E.tensor_scalar(
                        out=a1.t[:, 0:T], in0=gb.t[:, 2:2 + T], scalar1=cw(2), scalar2=cw(3),
                        op0=ALU.mult, op1=ALU.add))
                    cx.op("dve", [gb, convw, a1], [a1], lambda E, gb=gb, a1=a1, cw=cw: E.scalar_tensor_tensor(
                        out=a1.t[:, 0:T], in0=gb.t[:, 1:1 + T], scalar=cw(1), in1=a1.t[:, 0:T],
                        op0=ALU.mult, op1=ALU.add))
                    cx.op("dve", [gb, convw, a1], [a1], lambda E, gb=gb, a1=a1, cw=cw: E.scalar_tensor_tensor(
                        out=a1.t[:, 0:T], in0=gb.t[:, 0:T], scalar=cw(0), in1=a1.t[:, 0:T],
                        op0=ALU.mult, op1=ALU.add))
                    cx.op("act", [a1], [sg], lambda E, a1=a1, sg=sg: E.activation(
                        out=sg.t[:, 0:T], in_=a1.t[:, 0:T], func=AF.Silu))
                    cx.op("dve", [sg, pu], [hidT], lambda E, sg=sg, pu=pu, hp=hp: E.tensor_tensor(
                        out=hidT3[:, hp, 0:T], in0=sg.t[:, 0:T], in1=pu.t[:, 0:T], op=ALU.mult))
                if halo:
                    continue
                dense_tok(nsub, w_down, half * 43, 43, lambda k, s: hidT3[:, k, s * 128:(s + 1) * 128])
            if halo:
                return
            fbc = hidT.t[:].bitcast(F32)[:, 0:D]
            cx.dma("sp", fbc, fnorm_d.partition_broadcast(128), [], [hidT])
            for s in range(nsub):
                r = row0 - 128 + s * 128
                hb = hbs.next()
                ss, ms, sd, rstd = smalls.next(), smalls.next(), smalls.next(), smalls.next()
                cx.op("act", [xs[s]], [hb, ss], lambda E, hb=hb, ss=ss, s=s: E.activation(
                    out=hb.t[:], in_=xs[s].t[:], func=AF.Square, accum_out=ss.t[:, 0:1]))
                emit_rstd(cx, ss, ms, sd, rstd, D)
                cx.op("dve", [rstd, flags], [rstd], lambda E, rstd=rstd: E.tensor_scalar(
                    out=rstd.t[:, 0:1], in0=rstd.t[:, 0:1], scalar1=flags.t[:, 1:2], scalar2=flags.t[:, 0:1],
                    op0=ALU.mult, op1=ALU.add))
                cx.op("dve", [xs[s], rstd, hidT], [xs[s]], lambda E, rstd=rstd, s=s, fbc=fbc: E.scalar_tensor_tensor(
                    out=xs[s].t[:], in0=xs[s].t[:], scalar=rstd.t[:, 0:1], in1=fbc, op0=ALU.mult, op1=ALU.mult))
                cx.dma("sp", y[r:r + 128, :], xs[s].t[:], [xs[s], outbuf], [])

        do_block(0, 1, True)
        ntile = ntok // 512
        for ti in range(ntile):
            do_block(128 + ti * 512, 4, False)
        cx.finish([outbuf])
    return nc


def _colT(v):
    return np.ascontiguousarray(v.reshape(-1, 128).T)


def k2_consts(inp, l, final):
    gm = np.concatenate([np.ones(2048, np.float32), inp["swa_norm"][l].astype(np.float32), np.ones(1024, np.float32)])
    gains = np.concatenate([_colT(gm), _colT(inp["ffn_norm"][l])], axis=1).astype(np.float32)
    cw = np.concatenate([inp["ffn_conv_w"][l], inp["ffn_conv_b"][l][None, :]], axis=0)
    convw = np.ascontiguousarray(cw.reshape(4, NHT, 128).transpose(2, 1, 0).reshape(128, NHT * 4)).astype(np.float32)
    return {
        "w_out": np.ascontiguousarray(inp["w_out"][l]), "w_gate": np.ascontiguousarray(inp["w_gate"][l]),
        "w_up": np.ascontiguousarray(inp["w_up"][l]), "w_down": np.ascontiguousarray(inp["w_down"][l]),
        "gains": np.ascontiguousarray(gains), "convw": convw, "ident": np.eye(128, dtype=np.float32),
        "fnorm": (np.ascontiguousarray(inp["final_norm"]).astype(np.float32) if final else np.ones(D, np.float32)),
        "flags": np.ascontiguousarray(np.tile(np.array([[0.0, 1.0]] if final else [[1.0, 0.0]], np.float32), (128, 1))),
    }


def k2_shard(xflat, mixflat, c, ntok, seq):
    r0 = c * ntok
    xs = np.zeros((128 + ntok, D), np.float32)
    ms = np.zeros((128 + ntok, D), np.float32)
    xs[128:] = xflat[r0:r0 + ntok]
    ms[128:] = mixflat[r0:r0 + ntok]
    if r0 % seq != 0:
        xs[:128] = xflat[r0 - 128:r0]
        ms[:128] = mixflat[r0 - 128:r0]
    return xs, ms


NF = 14
TB = [(NF * 128, 512), (NF * 128 + 512, 456), (NF * 128 + 968, 256)]
NCOL1 = NF * 128 + 1224
C_ID, C_U, C_LST, C_ONE, C_BU, C_BL = 0, 128, 256, 384, 512, 640
C_GAIN, C_CONV, C_DTB, C_ALOG, C_DSK, C_SNW, C_SINK, C_BG, C_GNW = 768, 800, 840, 848, 856, 1368, 1880, 1884, 2012
CW1 = 2268


def build_k1(S):
    nsup = S // 512
    nc = bass.Bass("TRN2", target_bir_lowering=False)
    x = nc.dram_tensor("x", [S, D], F32, kind="ExternalInput").ap()
    w = nc.dram_tensor("w", [D, NCOL1], F32, kind="ExternalInput").ap()
    cst_d = nc.dram_tensor("cst", [128, CW1], F32, kind="ExternalInput").ap()
    bias_d = nc.dram_tensor("bias", [128, 2048], F32, kind="ExternalInput").ap()
    wgl_d = nc.dram_tensor("wgl", [16, 128], F32, kind="ExternalInput").ap()
    out = nc.dram_tensor("o", [S, 1024], F32, kind="ExternalOutput").ap()

    with ExitStack() as es:
        cx = Ctx(nc, es)
        sb = cx.sbuf
        xin = sb([128, D], F32, "xin")
        hb = sb([128, D], BF16, "hb")
        hT = sb([128, 32, 512], BF16, "hT")
        wring = Ring([sb([128, 4096], BF16, "wp%d" % i) for i in range(3)])
        convb = [sb([128, 515], F32, "cvb%d" % i) for i in range(8)]
        xcT = [sb([128, 512], BF16, "xcT%d" % i) for i in range(4)]
        BT = [sb([128, 512], BF16, "BT%d" % i) for i in range(2)]
        CT = [sb([128, 512], BF16, "CT%d" % i) for i in range(2)]
        qT = [sb([128, 512], BF16, "qT%d" % i) for i in range(4)]
        kkT = sb([128, 640], BF16, "kkT")
        gqT = sb([128, 512], F32, "gqT")
        gkT = sb([128, 512], F32, "gkT")
        lrT = sb([128, 512], F32, "lrT")
        tokb = [[sb([128, n], F32, "tok%d_%d" % (s, i)) for i, (_, n) in enumerate(TB)] for s in range(4)]
        cst = sb([128, CW1], F32, "cst")
        wgl = sb([16, 128], F32, "wgl")
        mbP = sb([128, 512], F32, "mbP")
        mbO = sb([128, 512], F32, "mbO")
        identb = sb([128, 128], BF16, "identb")
        ostage = Ring([sb([128, 1024], F32, "ost%d" % i) for i in range(2)])
        smalls = Ring([sb([128, 8], F32, "sm%d" % i) for i in range(24)])
        negA = sb([128, 8], F32, "negA")
        esink = sb([128, 4], F32, "esink")
        Ua = sb([128, 512], F32, "Ua")
        Ebuf = sb([128, 512], F32, "Ebuf")
        cbm = sb([128, 128], F32, "cbm")
        MT = sb([128, 512], BF16, "MT")
        xd = sb([128, 256], BF16, "xd")
        xD = sb([128, 256], F32, "xD")
        Btok = sb([128, 128], BF16, "Btok")
        xdd = sb([128, 256], BF16, "xdd")
        ty = sb([128, 256], F32, "ty")
        sz = sb([128, 256], F32, "sz")
        junk = sb([128, 256], F32, "junk")
        state = [sb([128, 256], F32, "st%d" % i) for i in range(2)]
        stateb = [sb([128, 256], BF16, "stb%d" % i) for i in range(2)]
        v1 = [sb([128, 65], BF16, "v1_%d" % i) for i in range(5)]
        tP = sb([128, 512], F32, "tP")
        pP = sb([128, 512], BF16, "pP")
        pO = sb([128, 512], BF16, "pO")
        xg = sb([128, 128], F32, "xg")
        la = sb([128, 128], F32, "la")
        edecT = sb([128, 128], F32, "edecT")
        einvT = sb([128, 128], F32, "einvT")
        qdA = sb([128, 128], BF16, "qdA")
        qdB = sb([128, 128], BF16, "qdB")
        kinvT = sb([128, 128], BF16, "kinvT")
        erem = sb([128, 128], F32, "erem")
        kend = sb([128, 128], BF16, "kend")
        kend1 = sb([128, 128], BF16, "kend1")
        vb = sb([128, 256], BF16, "vb")
        attm = sb([128, 128], BF16, "attm")
        SA = sb([128, 256], F32, "SA")
        SB = sb([128, 256], F32, "SB")
        S0b = sb([128, 256], BF16, "S0b")
        S1b = sb([128, 256], BF16, "S1b")
        on = sb([128, 256], F32, "on")
        sg = sb([128, 256], F32, "sg")
        outbuf = sb([1, 4], F32, "outdummy")

        pr = Ring([cx.psum([128, 512], F32, "pr%d" % i) for i in range(6)])
        tps = Ring([cx.psum([128, 1024], BF16, "tp%d" % i) for i in range(2)])

        def C(c0, n):
            return cst.t[:, c0:c0 + n]

        def b3(ap, a, b):
            return ap.unsqueeze(2).broadcast_to([128, a, b])

        def v3(ap, a):
            return ap.rearrange("p (a b) -> p a b", a=a)

        cx.dma("sp", cst.t[:], cst_d, [], [cst])
        cx.dma("sp", wgl.t[:], wgl_d, [], [wgl])
        cx.dma("pool", identb.t[:], cst_d[:, C_ID:C_ID + 128], [], [identb])
        cx.dma("sp", mbP.t[:], bias_d[:, 0:512], [], [mbP])
        cx.dma("sp", mbO.t[:], bias_d[:, 512:1024], [], [mbO])
        cx.dma("sp", tP.t[:], bias_d[:, 1024:1536], [], [tP])
        cx.op("dve", [mbP, tP], [mbP], lambda E: E.tensor_tensor(out=mbP.t[:], in0=mbP.t[:], in1=tP.t[:], op=ALU.add))
        cx.dma("sp", tP.t[:], bias_d[:, 1536:2048], [], [tP])
        cx.op("dve", [mbO, tP], [mbO], lambda E: E.tensor_tensor(out=mbO.t[:], in0=mbO.t[:], in1=tP.t[:], op=ALU.add))
        cx.op("act", [cst], [negA], lambda E: E.activation(out=negA.t[:], in_=C(C_ALOG, 8), func=AF.Exp))
        cx.op("dve", [negA], [negA], lambda E: E.tensor_scalar(out=negA.t[:], in0=negA.t[:], scalar1=-1.0, scalar2=None,
                                                              op0=ALU.mult))
        cx.op("act", [cst], [esink], lambda E: E.activation(out=esink.t[:], in_=C(C_SINK, 4), func=AF.Exp))
        for b_ in convb:
            cx.op("dve", [], [b_], lambda E, b_=b_: E.memset(b_.t[:, 0:3], 0.0))
        for b_ in state + [SA, SB]:
            cx.op("dve", [], [b_], lambda E, b_=b_: E.memset(b_.t[:], 0.0))
        for b_ in stateb + [S0b, S1b, qdA, qdB, kkT, kend, kend1] + qT:
            cx.op("dve", [], [b_], lambda E, b_=b_: E.memset(b_.t[:], 0.0))
        for b_ in v1:
            cx.op("dve", [], [b_], lambda E, b_=b_: E.memset(b_.t[:], 1.0))

        def wview(wp, k):
            return wp.t[:].rearrange("p (k n) -> p k n", k=k)

        def rstd_of(ss, dim):
            ms, rstd = smalls.next(), smalls.next()
            cx.op("dve", [ss], [ms], lambda E: E.tensor_scalar(out=ms.t[:, 0:1], in0=ss.t[:, 0:1], scalar1=1.0 / dim,
                                                               scalar2=EPS, op0=ALU.mult, op1=ALU.add))
            cx.op("act", [ms], [ms], lambda E: E.activation(out=ms.t[:, 0:1], in_=ms.t[:, 0:1], func=AF.Ln))
            cx.op("act", [ms], [rstd], lambda E: E.activation(out=rstd.t[:, 0:1], in_=ms.t[:, 0:1], func=AF.Exp, scale=-0.5))
            return rstd

        def sigmoid_of(dst, src_buf, src_ap):
            cx.op("act", [src_buf], [dst], lambda E: E.activation(out=dst.t[:], in_=src_ap, func=AF.Exp, scale=-1.0))
            cx.op("act", [dst], [dst], lambda E: E.activation(out=dst.t[:], in_=dst.t[:], func=AF.Ln, bias=1.0, scale=1.0))
            cx.op("act", [dst], [dst], lambda E: E.activation(out=dst.t[:], in_=dst.t[:], func=AF.Exp, scale=-1.0))

        def ssd_chunk(c, ost):
            cs = slice(c * 128, (c + 1) * 128)
            t1 = tokb[c][1]
            tmpa, tmpb, dt8, a8 = smalls.next(), smalls.next(), smalls.next(), smalls.next()
            cx.op("dve", [t1, cst], [tmpa], lambda E: E.tensor_tensor(out=tmpa.t[:, 0:8], in0=t1.t[:, 0:8],
                                                                    in1=C(C_DTB, 8), op=ALU.add))
            cx.op("act", [tmpa], [tmpb], lambda E: E.activation(out=tmpb.t[:, 0:8], in_=tmpa.t[:, 0:8], func=AF.Exp))
            cx.op("act", [tmpb], [dt8], lambda E: E.activation(out=dt8.t[:, 0:8], in_=tmpb.t[:, 0:8], func=AF.Ln,
                                                              bias=1.0, scale=1.0))
            cx.op("dve", [dt8, negA], [a8], lambda E: E.tensor_tensor(out=a8.t[:, 0:8], in0=dt8.t[:, 0:8],
                                                                     in1=negA.t[:, 0:8], op=ALU.mult))
            for g in range(2):
                g4 = slice(g * 4, g * 4 + 4)
                gc = slice(g * 256, (g + 1) * 256)
                cx.op("dve", [cst, a8], [Ua], lambda E: E.tensor_tensor(
                    out=v3(Ua.t[:], 4), in0=C(C_U, 128).unsqueeze(1).broadcast_to([128, 4, 128]),
                    in1=b3(a8.t[:, g4], 4, 128), op=ALU.mult))
                p1 = pr.next()
                cx.op("pe", [cst, Ua], [p1], lambda E: E.matmul(p1.t[:], lhsT=C(C_LST, 128), rhs=Ua.t[:], start=True, stop=True))
                cx.op("act", [p1], [Ebuf], lambda E: E.activation(out=Ebuf.t[:], in_=p1.t[:], func=AF.Exp))
                p2 = pr.next()
                cx.op("pe", [cst, a8], [p2], lambda E: [
                    E.matmul(p2.t[:, 0:4], lhsT=C(C_U, 128), rhs=a8.t[:, g4], start=True, stop=True),
                    E.matmul(p2.t[:, 4:8], lhsT=C(C_ONE, 128), rhs=a8.t[:, g4], start=True, stop=True)])
                ea8 = smalls.next()
                cx.op("act", [p2], [ea8], lambda E: E.activation(out=ea8.t[:, 0:8], in_=p2.t[:, 0:8], func=AF.Exp))
                p3 = pr.next()
                cx.op("pe", [BT[g], CT[g]], [p3], lambda E: E.matmul(p3.t[:, 0:128], lhsT=BT[g].t[:, cs], rhs=CT[g].t[:, cs],
                                                                    start=True, stop=True))
                cx.op("dve", [p3, cst], [cbm], lambda E: E.tensor_tensor(out=cbm.t[:], in0=p3.t[:, 0:128], in1=C(C_U, 128),
                                                                        op=ALU.mult))
                cx.op("dve", [Ebuf, cbm], [MT], lambda E: E.tensor_tensor(
                    out=v3(MT.t[:], 4), in0=v3(Ebuf.t[:], 4), in1=cbm.t[:].unsqueeze(1).broadcast_to([128, 4, 128]),
                    op=ALU.mult))
                tp = tps.next()
                cx.op("pe", [xcT[2 * g], xcT[2 * g + 1], BT[g], identb], [tp], lambda E: [
                    E.transpose(out=tp.t[:, 0:128], in_=xcT[2 * g].t[:, cs], identity=identb.t[:]),
                    E.transpose(out=tp.t[:, 128:256], in_=xcT[2 * g + 1].t[:, cs], identity=identb.t[:]),
                    E.transpose(out=tp.t[:, 256:384], in_=BT[g].t[:, cs], identity=identb.t[:])])
                cx.op("dve", [tp, dt8], [xd], lambda E: E.tensor_tensor(
                    out=v3(xd.t[:], 4), in0=v3(tp.t[:, 0:256], 4), in1=b3(dt8.t[:, g4], 4, 64), op=ALU.mult))
                cx.op("dve", [tp, cst], [xD], lambda E: E.tensor_tensor(
                    out=xD.t[:], in0=tp.t[:, 0:256], in1=C(C_DSK + g * 256, 256), op=ALU.mult))
                cx.op("act", [tp], [Btok], lambda E: E.activation(out=Btok.t[:], in_=tp.t[:, 256:384], func=AF.Copy))
                cx.op("dve", [xd, Ebuf], [xdd], lambda E: E.tensor_tensor(
                    out=v3(xdd.t[:], 4), in0=v3(xd.t[:], 4), in1=b3(v3(Ebuf.t[:], 4)[:, :, 127], 4, 64), op=ALU.mult))
                p4 = pr.next()
                cx.op("pe", [MT, xd], [p4], lambda E: [
                    E.matmul(p4.t[:, h * 64:(h + 1) * 64], lhsT=MT.t[:, h * 128:(h + 1) * 128],
                             rhs=xd.t[:, h * 64:(h + 1) * 64], start=True, stop=True) for h in range(4)])
                p5 = pr.next()
                cx.op("pe", [CT[g], stateb[g]], [p5], lambda E: E.matmul(
                    p5.t[:, 0:256], lhsT=CT[g].t[:, cs], rhs=stateb[g].t[:], start=True, stop=True))
                cx.op("dve", [p5, ea8], [ty], lambda E: E.tensor_tensor(
                    out=v3(ty.t[:], 4), in0=v3(p5.t[:, 0:256], 4), in1=b3(ea8.t[:, 0:4], 4, 64), op=ALU.mult))
                cx.op("dve", [ty, p4], [ty], lambda E: E.tensor_tensor(out=ty.t[:], in0=ty.t[:], in1=p4.t[:, 0:256], op=ALU.add))
                cx.op("dve", [ty, xD], [ty], lambda E: E.tensor_tensor(out=ty.t[:], in0=ty.t[:], in1=xD.t[:], op=ALU.add))
                z = tokb[c][0]
                sigmoid_of(sz, z, z.t[:, gc])
                cx.op("dve", [ty, z], [ty], lambda E: E.tensor_tensor(out=ty.t[:], in0=ty.t[:], in1=z.t[:, gc], op=ALU.mult))
                cx.op("dve", [ty, sz], [ty], lambda E: E.tensor_tensor(out=ty.t[:], in0=ty.t[:], in1=sz.t[:], op=ALU.mult))
                ss = smalls.next()
                cx.op("act", [ty], [junk, ss], lambda E: E.activation(out=junk.t[:], in_=ty.t[:], func=AF.Square,
                                                                     accum_out=ss.t[:, 0:1]))
                rstd = rstd_of(ss, 256)
                cx.op("dve", [ty, rstd, cst], [ost], lambda E: E.scalar_tensor_tensor(
                    out=ost.t[:, gc], in0=ty.t[:], scalar=rstd.t[:, 0:1], in1=C(C_SNW + g * 256, 256),
                    op0=ALU.mult, op1=ALU.mult))
                p6 = pr.next()
                cx.op("pe", [Btok, xdd], [p6], lambda E: E.matmul(p6.t[:, 0:256], lhsT=Btok.t[:], rhs=xdd.t[:],
                                                                 start=True, stop=True))
                cx.op("dve", [state[g], ea8], [state[g]], lambda E: E.tensor_tensor(
                    out=v3(state[g].t[:], 4), in0=v3(state[g].t[:], 4), in1=b3(ea8.t[:, 4:8], 4, 64), op=ALU.mult))
                cx.op("dve", [state[g], p6], [state[g]], lambda E: E.tensor_tensor(
                    out=state[g].t[:], in0=state[g].t[:], in1=p6.t[:, 0:256], op=ALU.add))
                cx.op("act", [state[g]], [stateb[g]], lambda E: E.activation(out=stateb[g].t[:], in_=state[g].t[:],
                                                                            func=AF.Copy))

        def swa_chunk(c, first, ost):
            cs = slice(c * 128, (c + 1) * 128)
            prev = slice(c * 128, (c + 1) * 128)
            own = slice(128 + c * 128, 256 + c * 128)
            t1 = tokb[c][1]
            cx.op("act", [t1], [v1[c + 1]], lambda E: E.activation(out=v1[c + 1].t[:, 0:64], in_=t1.t[:, 8:72], func=AF.Copy))
            pv = pr.next()
            halves = ([] if first else [(prev, mbP, pP, v1[c])]) + [(own, mbO, pO, v1[c + 1])]
            for (ks, mb, pp, _) in halves:
                ps = pr.next()

                def mm(E, ps=ps, ks=ks):
                    r = None
                    for h in range(4):
                        r = E.matmul(ps.t[:, h * 128:(h + 1) * 128], lhsT=kkT.t[:, ks], rhs=qT[h].t[:, cs],
                                     start=True, stop=True)
                    return r
                cx.op("pe", [kkT] + qT, [ps], mm)
                cx.op("dve", [ps, mb], [tP], lambda E, ps=ps, mb=mb: E.scalar_tensor_tensor(
                    out=tP.t[:], in0=ps.t[:], scalar=0.125, in1=mb.t[:], op0=ALU.mult, op1=ALU.add))
                cx.op("act", [tP], [pp], lambda E, pp=pp: E.activation(out=pp.t[:], in_=tP.t[:], func=AF.Exp))

            def mo(E):
                r = None
                for h in range(4):
                    for i, (_, _, pp, vv) in enumerate(halves):
                        r = E.matmul(pv.t[:, h * 65:(h + 1) * 65], lhsT=pp.t[:, h * 128:(h + 1) * 128], rhs=vv.t[:],
                                     start=(i == 0), stop=(i == len(halves) - 1))
                return r
            cx.op("pe", [pP, pO, v1[c], v1[c + 1]], [pv], mo)
            den, rden = smalls.next(), smalls.next()
            pv3 = pv.t[:, 0:260].rearrange("p (h d) -> p h d", h=4)
            cx.op("dve", [pv, esink], [den], lambda E: E.tensor_tensor(out=den.t[:, 0:4], in0=pv3[:, :, 64], in1=esink.t[:],
                                                                      op=ALU.add))
            cx.op("dve", [den], [rden], lambda E: E.reciprocal(out=rden.t[:, 0:4], in_=den.t[:, 0:4]))
            cx.op("dve", [pv, rden], [ost], lambda E: E.tensor_tensor(
                out=v3(ost.t[:, 512:768], 4), in0=pv3[:, :, 0:64], in1=b3(rden.t[:, 0:4], 4, 64), op=ALU.mult))

        def gla_chunk(c, ost):
            cs = slice(c * 128, (c + 1) * 128)
            t1 = tokb[c][1]
            gk_tok = t1.t[:, 72:200]
            gv_tok = t1.t[:, 200:456]
            g_tok = tokb[c][2].t[:, 0:256]
            p1 = pr.next()
            cx.op("pe", [lrT, wgl], [p1], lambda E: E.matmul(p1.t[:, 0:128], lhsT=lrT.t[0:16, cs], rhs=wgl.t[:],
                                                            start=True, stop=True))
            cx.op("dve", [p1, cst], [xg], lambda E: E.tensor_tensor(out=xg.t[:], in0=p1.t[:, 0:128], in1=C(C_BG, 128), op=ALU.add))
            cx.op("act", [xg], [xg], lambda E: E.activation(out=xg.t[:], in_=xg.t[:], func=AF.Exp, scale=-1.0))
            cx.op("act", [xg], [xg], lambda E: E.activation(out=xg.t[:], in_=xg.t[:], func=AF.Ln, bias=1.0, scale=1.0))
            cx.op("dve", [xg], [la], lambda E: E.tensor_scalar(out=la.t[:], in0=xg.t[:], scalar1=-1.0 / 16.0, scalar2=None,
                                                              op0=ALU.mult))
            p2 = pr.next()
            cx.op("pe", [la, cst], [p2], lambda E: E.matmul(p2.t[:, 0:128], lhsT=la.t[:], rhs=C(C_BU, 128), start=True, stop=True))
            p3 = pr.next()
            cx.op("pe", [la, cst], [p3], lambda E: E.matmul(p3.t[:, 0:128], lhsT=C(C_BL, 128), rhs=la.t[:], start=True, stop=True))
            cx.op("act", [p2], [edecT], lambda E: E.activation(out=edecT.t[:], in_=p2.t[:, 0:128], func=AF.Exp))
            cx.op("act", [p2], [einvT], lambda E: E.activation(out=einvT.t[:], in_=p2.t[:, 0:128], func=AF.Exp, scale=-1.0))
            cx.op("act", [p3], [erem], lambda E: E.activation(out=erem.t[:], in_=p3.t[:, 0:128], func=AF.Exp))
            sc = 128.0 ** -0.5
            cx.op("dve", [gqT, edecT], [qdA], lambda E: E.scalar_tensor_tensor(
                out=qdA.t[:, 0:64], in0=gqT.t[:, c * 128:c * 128 + 64], scalar=sc, in1=edecT.t[:, 0:64],
                op0=ALU.mult, op1=ALU.mult))
            cx.op("dve", [gqT, edecT], [qdB], lambda E: E.scalar_tensor_tensor(
                out=qdB.t[:, 64:128], in0=gqT.t[:, c * 128 + 64:c * 128 + 128], scalar=sc, in1=edecT.t[:, 64:128],
                op0=ALU.mult, op1=ALU.mult))
            cx.op("dve", [gkT, einvT], [kinvT], lambda E: E.tensor_tensor(out=kinvT.t[:], in0=gkT.t[:, cs], in1=einvT.t[:],
                                                                         op=ALU.mult))
            cx.op("dve", [t1, erem], [kend], lambda E: E.tensor_tensor(out=kend.t[0:64, :], in0=t1.t[0:64, 72:200],
                                                                      in1=erem.t[0:64, :], op=ALU.mult))
            cx.op("dve", [t1, erem], [kend1], lambda E: E.tensor_tensor(out=kend1.t[64:128, :], in0=t1.t[64:128, 72:200],
                                                                       in1=erem.t[64:128, :], op=ALU.mult))
            cx.op("act", [t1], [vb], lambda E: E.activation(out=vb.t[:], in_=gv_tok, func=AF.Copy))
            p4 = pr.next()
            cx.op("pe", [kinvT, qdA, qdB], [p4], lambda E: [
                E.matmul(p4.t[:, 0:64], lhsT=kinvT.t[:], rhs=qdA.t[:, 0:64], start=True, stop=True),
                E.matmul(p4.t[:, 64:128], lhsT=kinvT.t[:], rhs=qdB.t[:, 64:128], start=True, stop=True)])
            cx.op("dve", [p4, cst], [attm], lambda E: E.tensor_tensor(out=attm.t[:], in0=p4.t[:, 0:128], in1=C(C_BU, 128),
                                                                     op=ALU.mult))
            p5 = pr.next()
            cx.op("pe", [kend, vb], [p5], lambda E: E.matmul(p5.t[:, 0:256], lhsT=kend.t[:], rhs=vb.t[:],
                                                            start=True, stop=True))
            cx.op("dve", [SA, edecT, p5], [SB], lambda E: E.scalar_tensor_tensor(
                out=SB.t[:], in0=SA.t[:], scalar=edecT.t[:, 63:64], in1=p5.t[:, 0:256], op0=ALU.mult, op1=ALU.add))
            cx.op("act", [SB], [S1b], lambda E: E.activation(out=S1b.t[:], in_=SB.t[:], func=AF.Copy))
            p6 = pr.next()
            cx.op("pe", [attm, vb, qdA, qdB, S0b, S1b], [p6], lambda E: [
                E.matmul(p6.t[:, 0:256], lhsT=attm.t[:], rhs=vb.t[:], start=True, stop=False),
                E.matmul(p6.t[:, 0:256], lhsT=qdA.t[:], rhs=S0b.t[:], start=False, stop=False),
                E.matmul(p6.t[:, 0:256], lhsT=qdB.t[:], rhs=S1b.t[:], start=False, stop=True)])
            p7 = pr.next()
            cx.op("pe", [kend1, vb], [p7], lambda E: E.matmul(p7.t[:, 0:256], lhsT=kend1.t[:], rhs=vb.t[:],
                                                            start=True, stop=True))
            cx.op("dve", [SB, edecT, p7], [SA], lambda E: E.scalar_tensor_tensor(
                out=SA.t[:], in0=SB.t[:], scalar=edecT.t[:, 127:128], in1=p7.t[:, 0:256], op0=ALU.mult, op1=ALU.add))
            cx.op("act", [SA], [S0b], lambda E: E.activation(out=S0b.t[:], in_=SA.t[:], func=AF.Copy))
            ss = smalls.next()
            cx.op("act", [p6], [junk, ss], lambda E: E.activation(out=junk.t[:], in_=p6.t[:, 0:256], func=AF.Square,
                                                                 accum_out=ss.t[:, 0:1]))
            rstd = rstd_of(ss, 256)
            cx.op("dve", [p6, rstd, cst], [on], lambda E: E.scalar_tensor_tensor(
                out=on.t[:], in0=p6.t[:, 0:256], scalar=rstd.t[:, 0:1], in1=C(C_GNW, 256), op0=ALU.mult, op1=ALU.mult))
            sigmoid_of(sg, tokb[c][2], g_tok)
            cx.op("dve", [on, tokb[c][2]], [on], lambda E: E.tensor_tensor(out=on.t[:], in0=on.t[:], in1=g_tok, op=ALU.mult))
            cx.op("dve", [on, sg], [ost], lambda E: E.tensor_tensor(out=ost.t[:, 768:1024], in0=on.t[:], in1=sg.t[:], op=ALU.mult))

        for st in range(nsup):
            r0 = st * 512
            for s in range(4):
                cx.dma("sp", xin.t[:], x[r0 + s * 128:r0 + (s + 1) * 128, :], [], [xin])
                ss = smalls.next()
                cx.op("act", [xin], [hb, ss], lambda E, ss=ss: E.activation(out=hb.t[:], in_=xin.t[:], func=AF.Square,
                                                                           accum_out=ss.t[:, 0:1]))
                rstd = rstd_of(ss, D)
                cx.op("dve", [xin, rstd], [hb], lambda E, rstd=rstd: E.tensor_scalar(
                    out=hb.t[:], in0=xin.t[:], scalar1=rstd.t[:, 0:1], scalar2=None, op0=ALU.mult))
                gain = Buf(None, "g")
                emit_transposeT(cx, hb, hT, s, _CstView(cst, C_GAIN), identb, tps)
            for f in range(NF):
                wp = wring.next()
                cx.dma("pool", wview(wp, 32), w[:, f * 128:(f + 1) * 128].rearrange("(k p) n -> p k n", p=128), [], [wp])
                ps = pr.next()

                def mmf(E, wp=wp, ps=ps):
                    r = None
                    for k in range(32):
                        r = E.matmul(ps.t[:], lhsT=wview(wp, 32)[:, k, :], rhs=hT.t[:, k, :], start=(k == 0), stop=(k == 31))
                    return r
                cx.op("pe", [hT, wp], [ps], mmf)
                if f < 8:
                    dst, dap = convb[f], convb[f].t[:, 3:515]
                elif f < 10:
                    for e in range(2):
                        qd = qT[(f - 8) * 2 + e]
                        cx.op("act", [ps], [qd], lambda E, ps=ps, qd=qd, e=e: E.activation(
                            out=qd.t[e * 64:(e + 1) * 64, :], in_=ps.t[e * 64:(e + 1) * 64, :], func=AF.Copy))
                    continue
                elif f == 10:
                    dst, dap = kkT, kkT.t[:, 128:640]
                elif f == 11:
                    dst, dap = gqT, gqT.t[:]
                elif f == 12:
                    dst, dap = gkT, gkT.t[:]
                else:
                    dst, dap = lrT, lrT.t[:]
                cx.op("act", [ps], [dst], lambda E, ps=ps, dap=dap: E.activation(out=dap, in_=ps.t[:], func=AF.Copy))
            for bi, (c0, ncol) in enumerate(TB):
                accs = [pr.next() for _ in range(4)]
                for kc in range(4):
                    wp = wring.next()
                    cx.dma("pool", wview(wp, 8)[:, :, 0:ncol],
                           w[kc * 1024:(kc + 1) * 1024, c0:c0 + ncol].rearrange("(k p) n -> p k n", p=128), [], [wp])
                    for s in range(4):
                        def mmt(E, wp=wp, s=s, kc=kc):
                            r = None
                            for k in range(8):
                                r = E.matmul(accs[s].t[:, 0:ncol], lhsT=hT.t[:, kc * 8 + k, s * 128:(s + 1) * 128],
                                             rhs=wview(wp, 8)[:, k, 0:ncol], start=(kc == 0 and k == 0),
                                             stop=(kc == 3 and k == 7))
                            return r
                        cx.op("pe", [hT, wp], [accs[s]], mmt)
                for s in range(4):
                    cx.op("act", [accs[s]], [tokb[s][bi]], lambda E, s=s: E.activation(
                        out=tokb[s][bi].t[:], in_=accs[s].t[:, 0:ncol], func=AF.Copy))
            for f in range(8):
                cb = convb[f]
                cw = lambda i, f=f: C(C_CONV + f * 5 + i, 1)
                a1 = Ua if f % 2 == 0 else Ebuf
                cx.op("dve", [cb, cst], [a1], lambda E, cb=cb, a1=a1, cw=cw: E.tensor_scalar(
                    out=a1.t[:], in0=cb.t[:, 3:515], scalar1=cw(3), scalar2=cw(4), op0=ALU.mult, op1=ALU.add))
                for i in range(3):
                    cx.op("dve", [cb, cst, a1], [a1], lambda E, cb=cb, a1=a1, cw=cw, i=i: E.scalar_tensor_tensor(
                        out=a1.t[:], in0=cb.t[:, i:i + 512], scalar=cw(i), in1=a1.t[:], op0=ALU.mult, op1=ALU.add))
                dst = xcT[f] if f < 4 else (BT[f - 4] if f < 6 else CT[f - 6])
                cx.op("act", [a1], [dst], lambda E, a1=a1, dst=dst: E.activation(out=dst.t[:], in_=a1.t[:], func=AF.Silu))
                cx.op("dve", [cb], [cb], lambda E, cb=cb: E.tensor_copy(out=cb.t[:, 0:3], in_=cb.t[:, 512:515]))
            for c in range(4):
                ost = ostage.next()
                ssd_chunk(c, ost)
                swa_chunk(c, st == 0 and c == 0, ost)
                gla_chunk(c, ost)
                cx.dma("sp", out[r0 + c * 128:r0 + (c + 1) * 128, :], ost.t[:], [ost, outbuf], [])
            cx.op("dve", [kkT], [kkT], lambda E: E.tensor_copy(out=kkT.t[:, 0:128], in_=kkT.t[:, 512:640]))
            cx.op("dve", [v1[4]], [v1[0]], lambda E: E.tensor_copy(out=v1[0].t[:], in_=v1[4].t[:]))
        cx.finish([outbuf])
    return nc


class _CstView:
    class _T:
        def __init__(self, t, c0):
            self.t, self.c0 = t, c0

        def __getitem__(self, idx):
            p, c = idx
            return self.t[p, c.start + self.c0:c.stop + self.c0]

    def __init__(self, parent, c0):
        self.parent = parent
        self.t = _CstView._T(parent.t, c0)
        self.name = parent.name

    @property
    def w(self):
        return self.parent.w

    @property
    def r(self):
        return self.parent.r


def _t5_bucket_np(dist):
    d = np.maximum(dist.astype(np.float32), np.float32(1.0))
    large = 16 + (np.log(d / np.float32(16)) / np.float32(math.log(128 / 16)) * np.float32(16)).astype(np.int32)
    large = np.minimum(large, 31)
    return np.where(dist < 16, dist, large)


def k1_consts(inp, l, j):
    f32 = np.float32
    G = [2 * j, 2 * j + 1]
    w_in = inp["w_in"][l]
    cols = []
    for g in G:
        cols.append(np.arange(2048 + g * 256, 2048 + (g + 1) * 256))
    for g in G:
        cols.append(np.arange(4096 + g * 128, 4096 + (g + 1) * 128))
    for g in G:
        cols.append(np.arange(5120 + g * 128, 5120 + (g + 1) * 128))
    cols.append(np.arange(6176 + 4 * j * 64, 6176 + (4 * j + 4) * 64))
    kc = np.arange(7200 + j * 64, 7200 + (j + 1) * 64)
    cols += [kc, kc]
    cols.append(np.arange(7712 + j * 128, 7712 + (j + 1) * 128))
    cols.append(np.arange(8224 + j * 128, 8224 + (j + 1) * 128))
    colsF = np.concatenate(cols)
    wF = w_in[:, colsF]
    wlr = np.concatenate([w_in[:, 10784:10800], np.zeros((D, 112), f32)], axis=1)
    colsT = np.concatenate([
        np.arange(G[0] * 256, (G[0] + 1) * 256), np.arange(G[1] * 256, (G[1] + 1) * 256),
        np.arange(6144 + 4 * G[0], 6144 + 4 * G[0] + 4), np.arange(6144 + 4 * G[1], 6144 + 4 * G[1] + 4),
        np.arange(7456 + j * 64, 7456 + (j + 1) * 64),
        np.arange(8224 + j * 128, 8224 + (j + 1) * 128),
        np.arange(8736 + j * 256, 8736 + (j + 1) * 256),
        np.arange(9760 + j * 256, 9760 + (j + 1) * 256)])
    w = np.ascontiguousarray(np.concatenate([wF, wlr, w_in[:, colsT]], axis=1), dtype=f32)
    assert w.shape[1] == NCOL1

    cst = np.zeros((128, CW1), f32)
    ii = np.arange(128)
    cst[:, C_ID:C_ID + 128] = np.eye(128, dtype=f32)
    cst[:, C_U:C_U + 128] = (ii[:, None] <= ii[None, :])
    cst[:, C_LST:C_LST + 128] = (ii[:, None] > ii[None, :])
    cst[:, C_ONE:C_ONE + 128] = 1.0
    same = (ii[:, None] // 64) == (ii[None, :] // 64)
    cst[:, C_BU:C_BU + 128] = same & (ii[:, None] <= ii[None, :])
    cst[:, C_BL:C_BL + 128] = same & (ii[:, None] > ii[None, :])
    cst[:, C_GAIN:C_GAIN + 32] = _colT(inp["attn_norm"][l])
    chs = [G[0] * 256, G[0] * 256 + 128, G[1] * 256, G[1] * 256 + 128,
           2048 + G[0] * 128, 2048 + G[1] * 128, 3072 + G[0] * 128, 3072 + G[1] * 128]
    cwt, cbs = inp["ssd_conv_w"][l], inp["ssd_conv_b"][l]
    for f, ch in enumerate(chs):
        for i in range(4):
            cst[:, C_CONV + f * 5 + i] = cwt[i, ch:ch + 128]
        cst[:, C_CONV + f * 5 + 4] = cbs[ch:ch + 128]
    hs = np.concatenate([np.arange(4 * G[0], 4 * G[0] + 4), np.arange(4 * G[1], 4 * G[1] + 4)])
    cst[:, C_DTB:C_DTB + 8] = inp["ssd_dt_bias"][l][hs][None, :]
    cst[:, C_ALOG:C_ALOG + 8] = inp["ssd_a_log"][l][hs][None, :]
    cst[:, C_DSK:C_DSK + 512] = np.repeat(inp["ssd_d"][l][hs], 64)[None, :]
    cst[:, C_SNW:C_SNW + 512] = np.concatenate([inp["ssd_norm"][l][g * 256:(g + 1) * 256] for g in G])[None, :]
    cst[:, C_SINK:C_SINK + 4] = inp["swa_sinks"][l][4 * j:4 * j + 4][None, :]
    cst[:, C_BG:C_BG + 128] = inp["gla_b_gate"][l][j * 128:(j + 1) * 128][None, :]
    cst[:, C_GNW:C_GNW + 256] = inp["gla_norm"][l][None, :]

    qi = np.arange(128)[None, :]
    bias = np.zeros((128, 2048), f32)
    rb = inp["rel_bias"]
    for half in range(2):
        kj = np.arange(128)[:, None] + 128 * half
        dist = qi + 128 - kj
        valid = (dist >= 0) & (dist < 128)
        bk = _t5_bucket_np(np.clip(dist, 0, 127))
        for h in range(4):
            bias[:, half * 512 + h * 128: half * 512 + (h + 1) * 128] = rb[bk, 4 * j + h]
            bias[:, 1024 + half * 512 + h * 128: 1024 + half * 512 + (h + 1) * 128] = np.where(valid, 0.0, -30000.0)
    wgl = np.ascontiguousarray(inp["gla_w_gate"][l][:, j * 128:(j + 1) * 128], dtype=f32)
    return {"w": w, "cst": cst, "bias": bias, "wgl": wgl}


def k1_scatter(mix_b, o, j):
    mix_b[:, 2 * j * 256:(2 * j + 2) * 256] = o[:, 0:512]
    mix_b[:, 2048 + j * 256:2048 + (j + 1) * 256] = o[:, 512:768]
    mix_b[:, 3072 + j * 256:3072 + (j + 1) * 256] = o[:, 768:1024]


N2 = 8


def kernel(**inputs):
    import sys
    import time
    t0 = time.time()
    inp = {k: np.asarray(v) for k, v in inputs.items()}
    x = np.ascontiguousarray(inp["x"], dtype=np.float32)
    B, S, _ = x.shape
    xflat = x.reshape(B * S, D)
    ntok = (B * S) // N2
    for l in range(2):
        nc1 = build_k1(S)
        consts = [k1_consts(inp, l, j) for j in range(4)]
        in_maps = [{"x": xflat[(c // 4) * S:(c // 4 + 1) * S], **consts[c % 4]} for c in range(8)]
        res = run_bass_kernel_spmd(nc1, in_maps, core_ids=list(range(8)))
        mix = np.empty((B, S, D), np.float32)
        for c in range(8):
            k1_scatter(mix[c // 4], res.results[c]["o"], c % 4)
        del res, in_maps, consts
        print("[kernel] layer %d K1 done %.1fs" % (l, time.time() - t0), file=sys.stderr, flush=True)
        nc2 = build_k2(ntok)
        c2 = k2_consts(inp, l, l == 1)
        mixflat = mix.reshape(B * S, D)
        in_maps = []
        for c in range(N2):
            xs, ms = k2_shard(xflat, mixflat, c, ntok, S)
            in_maps.append({"x": xs, "mix": ms, **c2})
        res = run_bass_kernel_spmd(nc2, in_maps, core_ids=list(range(N2)))
        xflat = np.concatenate([r["y"] for r in res.results], axis=0)
        del res, in_maps, c2, mix, mixflat
        print("[kernel] layer %d K2 done %.1fs" % (l, time.time() - t0), file=sys.stderr, flush=True)
    return xflat.reshape(B, S, D)
```
